# Optimizing a Trainium2 kernel written in Bass

```python
import jax, jax.numpy as jnp
from jax import lax
import numpy as np

D_MODEL = 2048
BATCH = 2
SEQ = 8192
DEPTH = 1

GDN_HEADS = 8
GDN_DK = D_MODEL // 16
GDN_DV = D_MODEL // GDN_HEADS
GDN_QK = GDN_HEADS * GDN_DK
GDN_V = GDN_HEADS * GDN_DV
CONV_WIDTH = 4
RET_HEADS = 8
RET_DK = D_MODEL // 16
RET_DV = D_MODEL // RET_HEADS
RET_QK = RET_HEADS * RET_DK
RET_V = RET_HEADS * RET_DV
ROPE_BASE = 10000.0
CHUNK = 64
D_FF = ((8 * D_MODEL // 3 + 255) // 256) * 256
EPS = 1e-6
IN_SPLITS = (2 * GDN_QK + GDN_V, GDN_V, GDN_HEADS, GDN_HEADS,
             RET_QK, RET_QK, RET_V, RET_V, D_MODEL, D_MODEL)
N_IN = sum(IN_SPLITS)

kernel_name = "hybrid_gdn_retention_gated_merge"


def rmsnorm(x, w):
    xf = x.astype(jnp.float32)
    xf = xf * lax.rsqrt(jnp.mean(xf * xf, axis=-1, keepdims=True) + EPS)
    return xf.astype(x.dtype) * w


def group_norm_heads(o, w):
    mu = jnp.mean(o, axis=-1, keepdims=True)
    var = jnp.mean(jnp.square(o - mu), axis=-1, keepdims=True)
    return (o - mu) * lax.rsqrt(var + EPS) * w.reshape(o.shape[2], o.shape[3]).astype(jnp.float32)


def l2norm(x):
    return x * lax.rsqrt(jnp.sum(x * x, axis=-1, keepdims=True) + EPS)


def split_cols(p):
    offs = [int(o) for o in np.cumsum(IN_SPLITS)[:-1]]
    return jnp.split(p, offs, axis=-1)


def causal_conv(x, w):
    width, s = w.shape[0], x.shape[1]
    xp = jnp.pad(x, ((0, 0), (width - 1, 0), (0, 0)))
    out = xp[:, 0:s] * w[0]
    for j in range(1, width):
        out = out + xp[:, j:j + s] * w[j]
    return out


def rotary(x):
    d, s = x.shape[-1], x.shape[1]
    inv = ROPE_BASE ** (-jnp.arange(0, d, 2, dtype=jnp.float32) / d)
    ang = jnp.arange(s, dtype=jnp.float32)[:, None] * inv[None, :]
    cos = jnp.cos(ang)[None, :, None, :]
    sin = jnp.sin(ang)[None, :, None, :]
    x1, x2 = x[..., : d // 2], x[..., d // 2:]
    return jnp.concatenate([x1 * cos - x2 * sin, x2 * cos + x1 * sin], axis=-1)


def to_chunks(t):
    b, s, h, d = t.shape
    return t.reshape(b, s // CHUNK, CHUNK, h, d).transpose(0, 3, 1, 2, 4)


def to_chunks_scalar(t):
    b, s, h = t.shape
    return t.reshape(b, s // CHUNK, CHUNK, h).transpose(0, 3, 1, 2)


def from_scan(o):
    n, b, h, c, d = o.shape
    return o.transpose(1, 0, 3, 2, 4).reshape(b, n * c, h, d)


def gated_delta_rule(q, k, v, beta, g):
    c = q.shape[-2]
    incl = jnp.tril(jnp.ones((c, c), dtype=bool))
    strict = jnp.tril(jnp.ones((c, c), dtype=bool), -1)
    g = jnp.cumsum(g, axis=-1)
    decay = jnp.exp(jnp.where(incl, g[..., :, None] - g[..., None, :], -jnp.inf))
    kb = k * beta[..., None]
    a = jnp.where(strict, jnp.einsum('bhnid,bhnjd->bhnij', kb, k) * decay, 0.0)
    t = a + jnp.eye(c, dtype=a.dtype)
    u = lax.linalg.triangular_solve(t, v * beta[..., None], left_side=True, lower=True, unit_diagonal=True)
    w = lax.linalg.triangular_solve(t, kb * jnp.exp(g)[..., None], left_side=True, lower=True, unit_diagonal=True)
    attn = jnp.einsum('bhnid,bhnjd->bhnij', q, k) * decay

    def step(state, xs):
        q_c, k_c, u_c, w_c, g_c, attn_c = xs
        v_new = u_c - jnp.einsum('bhcd,bhde->bhce', w_c, state)
        o_c = (jnp.einsum('bhcd,bhde->bhce', q_c * jnp.exp(g_c)[..., None], state)
               + jnp.einsum('bhij,bhje->bhie', attn_c, v_new))
        g_last = g_c[..., -1:]
        state = (state * jnp.exp(g_last)[..., None]
                 + jnp.einsum('bhcd,bhce->bhde', k_c * jnp.exp(g_last - g_c)[..., None], v_new))
        return state, o_c

    b, h, _, _, dk = q.shape
    dv = v.shape[-1]
    s0 = jnp.zeros((b, h, dk, dv), jnp.float32)
    xs = tuple(jnp.moveaxis(z, 2, 0) for z in (q, k, u, w, g, attn))
    _, o = lax.scan(step, s0, xs)
    return from_scan(o)


def retention_chunked(q, k, v, log_gamma):
    c = q.shape[-2]
    pos = jnp.arange(c, dtype=jnp.float32)
    dist = pos[:, None] - pos[None, :]
    dmat = jnp.exp(jnp.where(dist >= 0, dist * log_gamma[:, None, None], -jnp.inf))
    scores = jnp.einsum('bhnid,bhnjd->bhnij', q, k) * dmat[:, None]
    intra = jnp.einsum('bhnij,bhnje->bhnie', scores, v)
    xi = jnp.exp((pos + 1.0) * log_gamma[:, None])[:, :, None]
    zeta = jnp.exp((c - 1.0 - pos) * log_gamma[:, None])[:, :, None]
    gamma_c = jnp.exp(c * log_gamma)[:, None, None]

    def step(state, xs):
        q_c, k_c, v_c = xs
        o_c = jnp.einsum('bhcd,bhde->bhce', q_c, state) * xi
        state = state * gamma_c + jnp.einsum('bhcd,bhce->bhde', k_c * zeta, v_c)
        return state, o_c

    b, h, _, _, dk = q.shape
    dv = v.shape[-1]
    s0 = jnp.zeros((b, h, dk, dv), jnp.float32)
    xs = tuple(jnp.moveaxis(z, 2, 0) for z in (q, k, v))
    _, inter = lax.scan(step, s0, xs)
    return from_scan(jnp.moveaxis(intra, 2, 0) + inter)


def setup_inputs(seed: int = 0) -> dict:
    key = jax.random.key(seed)
    ks = jax.random.split(key, 16)
    f32 = jnp.float32
    x = jax.random.normal(ks[0], (BATCH, SEQ, D_MODEL), f32)
    norm1_w = 1.0 + 0.02 * jax.random.normal(ks[1], (DEPTH, D_MODEL), f32)
    w_in = jax.random.normal(ks[2], (DEPTH, D_MODEL, N_IN), f32) * D_MODEL ** -0.5
    conv_w = jax.random.normal(ks[3], (DEPTH, CONV_WIDTH, 2 * GDN_QK + GDN_V), f32) * CONV_WIDTH ** -0.5
    a_log = jnp.log(jax.random.uniform(ks[4], (DEPTH, GDN_HEADS), f32, 1.0, 16.0))
    dt = jnp.exp(jax.random.uniform(ks[5], (DEPTH, GDN_HEADS), f32, float(np.log(1e-3)), float(np.log(1e-1))))
    dt_bias = dt + jnp.log(-jnp.expm1(-dt))
    gdn_norm_w = 1.0 + 0.02 * jax.random.normal(ks[6], (DEPTH, GDN_DV), f32)
    ret_norm_w = 1.0 + 0.02 * jax.random.normal(ks[7], (DEPTH, RET_V), f32)
    w_out = jax.random.normal(ks[8], (DEPTH, D_MODEL, D_MODEL), f32) * D_MODEL ** -0.5
    norm2_w = 1.0 + 0.02 * jax.random.normal(ks[9], (DEPTH, D_MODEL), f32)
    w_gate = jax.random.normal(ks[10], (DEPTH, D_MODEL, D_FF), f32) * D_MODEL ** -0.5
    w_up = jax.random.normal(ks[11], (DEPTH, D_MODEL, D_FF), f32) * D_MODEL ** -0.5
    w_down = jax.random.normal(ks[12], (DEPTH, D_FF, D_MODEL), f32) * D_FF ** -0.5
    norm_f_w = 1.0 + 0.02 * jax.random.normal(ks[13], (D_MODEL,), f32)
    return {"x": x, "norm1_w": norm1_w, "w_in": w_in, "conv_w": conv_w, "a_log": a_log,
            "dt_bias": dt_bias, "gdn_norm_w": gdn_norm_w, "ret_norm_w": ret_norm_w,
            "w_out": w_out, "norm2_w": norm2_w, "w_gate": w_gate, "w_up": w_up,
            "w_down": w_down, "norm_f_w": norm_f_w}


def reference(x, norm1_w, w_in, conv_w, a_log, dt_bias, gdn_norm_w, ret_norm_w,
              w_out, norm2_w, w_gate, w_up, w_down, norm_f_w):
    b, s, _ = x.shape
    f32 = jnp.float32
    log_gamma = jnp.log1p(-jnp.exp2(-5.0 - jnp.arange(RET_HEADS, dtype=f32)))
    h = x
    for l in range(DEPTH):
        u = rmsnorm(h, norm1_w[l])
        proj = u @ w_in[l]
        a_qkv, a_z, a_b, a_a, r_q, r_k, r_v, r_g, gate_a, gate_b = split_cols(proj)

        qkv = jax.nn.silu(causal_conv(a_qkv, conv_w[l])).astype(f32)
        q_a = l2norm(qkv[..., :GDN_QK].reshape(b, s, GDN_HEADS, GDN_DK)) * GDN_DK ** -0.5
        k_a = l2norm(qkv[..., GDN_QK:2 * GDN_QK].reshape(b, s, GDN_HEADS, GDN_DK))
        v_a = qkv[..., 2 * GDN_QK:].reshape(b, s, GDN_HEADS, GDN_DV)
        beta = jax.nn.sigmoid(a_b.astype(f32))
        g = -jnp.exp(a_log[l].astype(f32)) * jax.nn.softplus(a_a.astype(f32) + dt_bias[l].astype(f32))
        o_a = gated_delta_rule(to_chunks(q_a), to_chunks(k_a), to_chunks(v_a),
                               to_chunks_scalar(beta), to_chunks_scalar(g))
        o_a = rmsnorm(o_a, gdn_norm_w[l].astype(f32)) * jax.nn.silu(a_z.astype(f32).reshape(b, s, GDN_HEADS, GDN_DV))
        o_a = o_a.reshape(b, s, GDN_V).astype(x.dtype)

        q_b = rotary(r_q.astype(f32).reshape(b, s, RET_HEADS, RET_DK))
        k_b = rotary(r_k.astype(f32).reshape(b, s, RET_HEADS, RET_DK)) * RET_DK ** -0.5
        v_b = r_v.astype(f32).reshape(b, s, RET_HEADS, RET_DV)
        o_b = retention_chunked(to_chunks(q_b), to_chunks(k_b), to_chunks(v_b), log_gamma)
        o_b = group_norm_heads(o_b, ret_norm_w[l]).reshape(b, s, RET_V) * jax.nn.silu(r_g.astype(f32))
        o_b = o_b.astype(x.dtype)

        mixed = jax.nn.sigmoid(gate_a) * o_a + jax.nn.sigmoid(gate_b) * o_b
        h = h + mixed @ w_out[l]

        hn = rmsnorm(h, norm2_w[l])
        h = h + (jax.nn.silu(hn @ w_gate[l]) * (hn @ w_up[l])) @ w_down[l]
    return rmsnorm(h, norm_f_w)
```

```python
import numpy as np
import concourse.bass as bass
import concourse.mybir as mybir
from concourse.bass_utils import run_bass_kernel_spmd

F32 = mybir.dt.float32
BF16 = mybir.dt.bfloat16
ALU = mybir.AluOpType
AF = mybir.ActivationFunctionType
AX = mybir.AxisListType

D = 2048
DFF = 5632
NCORES = 8
EPS = 1e-6
SAME_ENGINE_SYNC = True
PSUM_EXCL = True
STATIC_TOK = False
NOCC = False
STOPA = False
NBLK_LIMIT = None
LEVEL = 99
SUB = 99


class Slot:
    def __init__(self, sched, name, step=16):
        self.step = step
        self.key = "dma_" + name
        self.sem = sched.nc.alloc_semaphore("q_" + name)
        self.count = 0
        sched.semof[self.key] = self.sem


class Sched:
    ENG = ("pe", "dve", "act", "pool", "sp")

    def __init__(self, nc):
        self.nc = nc
        self.streams = {k: [] for k in self.ENG}
        self.semof = {k: nc.alloc_semaphore("c_" + k) for k in self.ENG}
        self.cnt = {k: 0 for k in self.ENG}
        self.waited = {k: {} for k in self.ENG}
        self.state = {}
        self.bank_of = {}
        self.bank_keys = {}

    def set_banks(self, mapping):
        for k, b in mapping.items():
            self.bank_of[k] = b
            self.bank_keys.setdefault(b, []).append(k)

    def slot(self, name, step=16):
        return Slot(self, name, step)

    def emit(self, eng, fn, reads=(), writes=(), slot=None, inc=True):
        bo = self.bank_of
        reads = [("B%d" % bo[r]) if r in bo else r for r in reads]
        writes = [("B%d" % bo[w]) if w in bo else w for w in writes]
        deps = []
        for r in reads:
            st = self.state.get(r)
            if st and st[0] is not None:
                deps.append(st[0])
            if st and PSUM_EXCL and r[0] == "B" and r[1:].isdigit():
                deps.extend(tk for tk in st[1] if tk[0] != eng)
        for w in writes:
            st = self.state.get(w)
            if st:
                if st[0] is not None:
                    deps.append(st[0])
                deps.extend(st[1])
        wd = self.waited[eng]
        m = {}
        for (k, v) in deps:
            if k == eng and (eng == "pe" or not SAME_ENGINE_SYNC):
                continue
            if wd.get(k, 0) >= v:
                continue
            wd[k] = v
            m[k] = max(m.get(k, 0), v)
        waits = list(m.items())
        if slot is not None:
            slot.count += slot.step
            tok = (slot.key, slot.count)
            incspec = (slot.sem, slot.step)
        elif inc:
            self.cnt[eng] += 1
            tok = (eng, self.cnt[eng])
            incspec = (self.semof[eng], 1)
        else:
            tok = (eng, self.cnt[eng] + 1)
            incspec = None
        for r in reads:
            st = self.state.setdefault(r, [None, []])
            st[1].append(tok)
            if len(st[1]) > 24:
                mm = {}
                for k, v in st[1]:
                    mm[k] = max(mm.get(k, 0), v)
                st[1] = list(mm.items())
        for w in writes:
            self.state[w] = [tok, []]
        self.streams[eng].append((waits, fn, incspec))
        return tok

    def barrier(self):
        alltoks = {}
        for k in self.ENG:
            if self.cnt[k] > 0:
                alltoks[k] = self.cnt[k]
        for st in self.state.values():
            toks = list(st[1])
            if st[0] is not None:
                toks.append(st[0])
            for (k, v) in toks:
                if k.startswith("dma_"):
                    alltoks[k] = max(alltoks.get(k, 0), v)
        for eng in self.ENG:
            wd = self.waited[eng]
            waits = []
            for k, v in alltoks.items():
                if k == eng or wd.get(k, 0) >= v:
                    continue
                wd[k] = v
                waits.append((k, v))
            if waits:
                self.streams[eng].append((waits, None, None))

    def final_wait(self, eng, toks):
        m = {}
        for k, v in toks:
            m[k] = max(m.get(k, 0), v)
        self.streams[eng].append((list(m.items()), None, None))

    def replay(self):
        nc = self.nc
        sch = self

        def run(name, e):
            for waits, fn, incspec in sch.streams[name]:
                for (k, v) in waits:
                    e.wait_ge(sch.semof[k], v)
                if fn is None:
                    continue
                ins = fn(e)
                if incspec is not None:
                    ins.then_inc(incspec[0], incspec[1])

        with nc.Block() as block:
            @block.tensor
            def _(e):
                run("pe", e)

            @block.vector
            def _(e):
                run("dve", e)

            @block.scalar
            def _(e):
                run("act", e)

            @block.gpsimd
            def _(e):
                run("pool", e)

            @block.sync
            def _(e):
                run("sp", e)


class Arena:
    def __init__(self, nc, nbytes):
        self.t = nc.alloc_sbuf_tensor("arena", [128, nbytes // 4], F32)
        self.cap = nbytes
        self.off = 0

    def take(self, shape, dtype):
        n = int(np.prod(shape))
        sz = n * (4 if dtype == F32 else 2)
        sz = (sz + 31) // 32 * 32
        assert self.off + sz <= self.cap, ("arena overflow", self.off, sz, self.cap)
        v = self.t[:, self.off // 4:(self.off + sz) // 4]
        if dtype != F32:
            v = v.bitcast(dtype)
        v = v[:, 0:n]
        if len(shape) == 2:
            v = v.rearrange("p (a b) -> p a b", a=shape[0])
        elif len(shape) == 3:
            v = v.rearrange("p (a b c) -> p a b c", a=shape[0], b=shape[1])
        self.off += sz
        return v


C_ID, C_TRI, C_NEGM, C_STRICT, C_MASK, C_SAMEC, C_ONES, C_INVF, C_MISC, C_GNW, C_RNW, NCST = (
    0, 128, 256, 384, 512, 640, 768, 896, 960, 976, 1232, 1488)
M_POSC, M_ALOG, M_DTB, M_LG, M_POS4, M_CSEL, M_EPS, M_ONE = 0, 1, 2, 3, 4, 8, 10, 11

TWO_PI = 2.0 * np.pi
CW1 = 6.28125
CW2 = float(TWO_PI - 6.28125)
MAGIC = 12582912.0


def build(SEQ, debug=False):
    B = 2
    NTOK = B * SEQ
    TPC = NTOK // NCORES
    NBLK = NTOK // 512
    BPB = SEQ // 512
    nc = bass.Bass("TRN2", target_bir_lowering=False)
    S = Sched(nc)

    def din(name, shape, dt=F32):
        return nc.dram_tensor(name, list(shape), dt, kind="ExternalInput").ap()

    x = din("x", [NTOK, D])
    xb = din("xb", [TPC, D])
    wq_d = din("wq", [D, 512])
    wt_d = din("wt", [D, 1536])
    wba_d = din("wba", [D, 2])
    cw_d = din("cw", [128, 16])
    n1_d = din("n1", [128, 16])
    n2_d = din("n2", [128, 16])
    nf_d = din("nf", [128, D])
    cst_d = din("cst", [128, NCST])
    wo_d = din("w_out", [D, D])
    wg_d = din("w_gate", [D, DFF])
    wu_d = din("w_up", [D, DFF])
    wd_d = din("w_down", [DFF, D])
    out_d = nc.dram_tensor("out", [TPC, D], F32, kind="ExternalOutput").ap()
    ag_in = nc.dram_tensor("ag_in", [256, NTOK], BF16).ap()
    ag_out = nc.dram_tensor("ag_out", [2048, NTOK], BF16).ap()
    dbg_mixed = None
    if debug:
        dbg_mixed = nc.dram_tensor("dbg_mixed", [256, NTOK], BF16, kind="ExternalOutput").ap()

    S.set_banks({"pt": 0, "ptA": 5, "ptB": 5, "ptC": 6, "ptD": 6, "pGCB": 3, "pKK": 3, "pQK": 3, "pSC": 3,
                 "pN": 4, "pNT": 4, "pX": 4, "pgs": 4, "p1": 5, "p2": 5, "p5": 6, "pU": 6, "pOa": 7, "pOb": 7,
                 "pp1": 1, "pp2": 2, "pp3": 3, "pp4": 4, "pp5": 5, "pp6": 6, "pp7": 7})
    A = Arena(nc, 208000)
    banks = [nc.alloc_psum_tensor("pb%d" % i, [128, 512], F32) for i in range(8)]

    def pf(b, off, n):
        return banks[b][:, off:off + n]

    def pb16(b, off, n):
        return banks[b][:, :].bitcast(BF16)[:, off:off + n]

    cst = A.take([NCST], F32)
    identb = A.take([128], BF16)
    Wq = A.take([16, 512], BF16)
    Wt = A.take([16, 1536], BF16)
    Wba = A.take([16, 2], BF16)
    cw = A.take([16], F32)
    n1 = A.take([16], F32)
    cols = A.take([16], F32)
    XI, XIK, GC, NEGA, DTB = 0, 1, 2, 3, 4
    ident = cst[:, C_ID:C_ID + 128]
    tri = cst[:, C_TRI:C_TRI + 128]
    negm = cst[:, C_NEGM:C_NEGM + 128]
    strict = cst[:, C_STRICT:C_STRICT + 128]
    mask01 = cst[:, C_MASK:C_MASK + 128]
    samec = cst[:, C_SAMEC:C_SAMEC + 128]
    ones = cst[:, C_ONES:C_ONES + 128]
    invf = cst[:, C_INVF:C_INVF + 64]
    misc = cst[:, C_MISC:C_MISC + 16]
    gnw = cst[:, C_GNW:C_GNW + 256]
    rnw = cst[:, C_RNW:C_RNW + 256]

    sl = {n: S.slot(n) for n in ("cst", "small", "small2", "dbg", "xs0", "xs1", "wst0", "wst1", "wst2", "mts", "ag_in")}
    s_cc = S.slot("cc", step=1)

    def dma(eng, out, in_, slot, reads=(), writes=()):
        return S.emit(eng, lambda e: e.dma_start(out=out, in_=in_), reads=reads, writes=writes, slot=slot)

    dma("sp", cst, cst_d, sl["cst"], writes=["cst"])
    dma("sp", cw, cw_d, sl["small"], writes=["cw"])
    dma("sp", n1, n1_d, sl["small2"], writes=["n1"])
    S.emit("dve", lambda e: e.tensor_copy(out=identb, in_=ident), reads=["cst"], writes=["identb"])
    S.emit("dve", lambda e: e.memset(cols, 0.0), writes=["cols"])
    lgc = misc[:, M_LG:M_LG + 1]
    tmpc = A.take([8], F32)
    S.emit("dve", lambda e: e.tensor_scalar(out=tmpc[:, 0:1], in0=misc[:, M_POSC:M_POSC + 1], scalar1=lgc, scalar2=None,
                                            op0=ALU.mult), reads=["cst"], writes=["tmpc"])
    S.emit("dve", lambda e: e.tensor_scalar(out=tmpc[:, 1:2], in0=tmpc[:, 0:1], scalar1=-1.0,
                                            scalar2=float(np.log(128.0 ** -0.5)), op0=ALU.mult, op1=ALU.add),
           reads=["tmpc"], writes=["tmpc"])
    S.emit("dve", lambda e: e.tensor_scalar(out=tmpc[:, 2:3], in0=lgc, scalar1=64.0, scalar2=None, op0=ALU.mult),
           reads=["cst", "tmpc"], writes=["tmpc"])
    S.emit("dve", lambda e: e.tensor_copy(out=tmpc[:, 3:4], in_=misc[:, M_ALOG:M_ALOG + 1]), reads=["cst", "tmpc"],
           writes=["tmpc"])
    S.emit("act", lambda e: e.activation(out=cols[:, 0:4], in_=tmpc[:, 0:4], func=AF.Exp), reads=["tmpc", "cols"],
           writes=["cols"])
    S.emit("dve", lambda e: e.tensor_scalar(out=cols[:, NEGA:NEGA + 1], in0=cols[:, NEGA:NEGA + 1], scalar1=-1.0,
                                            scalar2=None, op0=ALU.mult), reads=["cols"], writes=["cols"])
    S.emit("dve", lambda e: e.tensor_copy(out=cols[:, DTB:DTB + 1], in_=misc[:, M_DTB:M_DTB + 1]), reads=["cst", "cols"],
           writes=["cols"])

    xs = [A.take([D], F32) for _ in range(2)]
    xn = A.take([D], BF16)
    uT = A.take([16, 512], BF16)
    pre = A.take([4, 515], F32)
    cv = A.take([4, 512], F32)
    sq = A.take([2, 512], F32)
    rn = A.take([2, 512], F32)
    ssb = A.take([8], F32)
    szt = A.take([256], F32)
    tgt = A.take([256], F32)
    qk_sb = A.take([2, 4, 128], F32)
    rt1 = A.take([2, 4, 128], F32)
    rt2 = A.take([2, 4, 128], F32)
    ang = A.take([4, 128], F32)
    ang2 = A.take([4, 128], F32)
    SC = A.take([4, 128], F32)
    pos4 = A.take([4], F32)
    bat = A.take([8], F32)
    inter = []
    for p in range(2):
        inter.append(dict(
            qTn=A.take([512], BF16), kTn=A.take([512], BF16), vT=A.take([2, 512], BF16),
            bcol=A.take([4], F32), nbcol=A.take([4], F32), gcol=A.take([4], F32),
            qx=A.take([4, 128], BF16), kz=A.take([4, 128], BF16), vb=A.take([4, 256], BF16),
            Ga=A.take([4, 256], F32), Gb=A.take([4, 256], F32)))
    Sa = A.take([256], F32)
    Sabf = A.take([256], BF16)
    Zs = A.take([256], F32)
    Zbf = A.take([256], BF16)
    kTc = [A.take([128], BF16) for _ in range(2)]
    qgTc = [A.take([128], BF16) for _ in range(2)]
    kdc = [A.take([128], BF16) for _ in range(2)]
    qxTc = [A.take([128], BF16) for _ in range(2)]
    kzc = [A.take([128], BF16) for _ in range(2)]
    zero_list = [("Sa", Sa), ("Sabf", Sabf), ("Zs", Zs), ("Zbf", Zbf)]
    for i in range(2):
        zero_list += [("kTc%d" % i, kTc[i]), ("qgTc%d" % i, qgTc[i]), ("kdc%d" % i, kdc[i]),
                      ("qxTc%d" % i, qxTc[i]), ("kzc%d" % i, kzc[i])]
    gB = A.take([128], F32)
    tmpD = A.take([128], F32)
    Ef = A.take([128], F32)
    Es = A.take([128], F32)
    EGCB = A.take([128], F32)
    Nn = [A.take([128], F32) for _ in range(2)]
    NT = [A.take([128], F32) for _ in range(2)]
    Xm = A.take([128], F32)
    M2bf = A.take([128], BF16)
    attnT = A.take([128], BF16)
    scT = A.take([128], BF16)
    qxT = A.take([128], BF16)
    kzT = A.take([128], BF16)
    vtok = A.take([256], BF16)
    Rbf = A.take([256], BF16)
    vnbf = A.take([256], BF16)
    ta = A.take([256], F32)
    tb = A.take([256], F32)
    mixbf = A.take([256], BF16)
    mTs = A.take([2, 512], BF16)
    gcs = A.take([8], F32)
    gsel = A.take([2], F32)
    ost = A.take([16], F32)
    junk = A.take([256], BF16)
    zero_list += [("Rbf", Rbf), ("vnbf", vnbf)]
    print("phase A arena bytes", A.off)

    for nm, t in zero_list:
        S.emit("pool", lambda e, t=t: e.memset(t, 0.0), writes=[nm])

    wq_v = wq_d.rearrange("(k p) n -> p k n", p=128)
    wt_v = wt_d.rearrange("(k p) n -> p k n", p=128)
    wba_v = wba_d.rearrange("(k p) n -> p k n", p=128)
    for kc in range(16):
        st0 = xs[0][:, 0:1536]
        st1 = xs[1][:, 0:512]
        st2 = xs[1][:, 512:514]
        dma("sp", st0, wt_v[:, kc, :], sl["wst0"], writes=["xs0"])
        dma("sp", st1, wq_v[:, kc, :], sl["wst1"], writes=["xs1"])
        dma("sp", st2, wba_v[:, kc, :], sl["wst2"], writes=["xs1b"])
        n1c = n1[:, kc:kc + 1]
        S.emit("dve", lambda e, kc=kc, n1c=n1c, st0=st0: e.tensor_scalar(out=Wt[:, kc, :], in0=st0, scalar1=n1c, scalar2=None,
                                                                    op0=ALU.mult), reads=["xs0", "n1"], writes=["Wt"])
        S.emit("act", lambda e, kc=kc, n1c=n1c, st1=st1: e.activation(out=Wq[:, kc, :], in_=st1, func=AF.Copy, scale=n1c),
               reads=["xs1", "n1"], writes=["Wq"])
        S.emit("dve", lambda e, kc=kc, n1c=n1c, st2=st2: e.tensor_scalar(out=Wba[:, kc, :], in0=st2, scalar1=n1c, scalar2=None,
                                                                    op0=ALU.mult), reads=["xs1b", "n1"], writes=["Wba"])
    S.emit("dve", lambda e: e.tensor_copy(out=ssb[:, 0:1], in_=cst[:, 0:1]), reads=["cst"], writes=["xs1", "xs1b", "ssb"])

    def stage1(b):
        p = b % 2
        I = inter[p]
        sfx = "_%d" % p
        tok0 = b * 512
        bstart = (b % BPB) * 512
        first_in_seq = (b % BPB == 0)
        if LEVEL < 2:
            return
        for t in range(4):
            xsb = xs[t % 2]
            xk = "xs%d" % (t % 2)
            dma("sp", xsb, x[tok0 + t * 128: tok0 + (t + 1) * 128, :], sl[xk], writes=[xk])
            S.emit("act", lambda e, xsb=xsb, t=t: e.activation(out=xn, in_=xsb, func=AF.Square, accum_out=ssb[:, t:t + 1]),
                   reads=[xk], writes=["xn", "ssb"])
            S.emit("dve", lambda e, t=t: e.tensor_scalar(out=ssb[:, 4 + t:5 + t], in0=ssb[:, t:t + 1], scalar1=1.0 / D,
                                                         scalar2=EPS, op0=ALU.mult, op1=ALU.add), reads=["ssb"], writes=["ssb"])
            S.emit("act", lambda e, t=t: e.activation(out=ssb[:, 4 + t:5 + t], in_=ssb[:, 4 + t:5 + t], func=AF.Ln),
                   reads=["ssb"], writes=["ssb"])
            S.emit("act", lambda e, t=t: e.activation(out=ssb[:, 4 + t:5 + t], in_=ssb[:, 4 + t:5 + t], func=AF.Exp, scale=-0.5),
                   reads=["ssb"], writes=["ssb"])
            S.emit("act", lambda e, xsb=xsb, t=t: e.activation(out=xn, in_=xsb, func=AF.Copy, scale=ssb[:, 4 + t:5 + t]),
                   reads=[xk, "ssb"], writes=["xn"])
            for r in range(4):
                for q in range(4):
                    kc = r * 4 + q
                    S.emit("pe", lambda e, kc=kc, q=q: e.transpose(out=pb16(0, q * 128, 128), in_=xn[:, kc * 128:(kc + 1) * 128],
                                                                    identity=identb),
                           reads=["xn", "identb"], writes=["pt"], inc=(q == 3))
                eng = "dve" if r % 2 == 0 else "act"
                src = pb16(0, 0, 512).rearrange("p (a b) -> p a b", a=4)
                dst = uT[:, r * 4:(r + 1) * 4, t * 128:(t + 1) * 128]
                if eng == "dve":
                    S.emit("dve", lambda e, src=src, dst=dst: e.tensor_copy(out=dst, in_=src), reads=["pt"], writes=["uT"])
                else:
                    S.emit("act", lambda e, src=src, dst=dst: e.activation(out=dst, in_=src, func=AF.Copy), reads=["pt"],
                           writes=["uT"])
            yield
        if LEVEL < 3:
            return
        if first_in_seq:
            S.emit("pool", lambda e: e.memset(pre[:, :, 0:3], 0.0), writes=["pre"])
        for cc in range(4):
            bank = 1 + (cc % 2)
            pk = "pp%d" % bank
            for kc in range(16):
                S.emit("pe", lambda e, kc=kc, cc=cc, bank=bank: e.matmul(pf(bank, 0, 512), lhsT=Wq[:, kc, cc * 128:(cc + 1) * 128],
                                                                          rhs=uT[:, kc, :], start=(kc == 0), stop=(kc == 15)),
                       reads=["Wq", "uT"], writes=[pk], inc=(kc == 15))
            S.emit("act", lambda e, cc=cc, bank=bank: e.activation(out=pre[:, cc, 3:515], in_=pf(bank, 0, 512), func=AF.Copy),
                   reads=[pk], writes=["pre"])
            yield
        for cc in range(4):
            S.emit("dve", lambda e, cc=cc: e.tensor_scalar(out=cv[:, cc, :], in0=pre[:, cc, 0:512], scalar1=cw[:, cc * 4:cc * 4 + 1],
                                                           scalar2=None, op0=ALU.mult), reads=["pre", "cw"], writes=["cv"])
            for j in range(1, 4):
                S.emit("dve", lambda e, cc=cc, j=j: e.scalar_tensor_tensor(out=cv[:, cc, :], in0=pre[:, cc, j:j + 512],
                                                                            scalar=cw[:, cc * 4 + j:cc * 4 + j + 1], in1=cv[:, cc, :],
                                                                            op0=ALU.mult, op1=ALU.add),
                       reads=["pre", "cw", "cv"], writes=["cv"])
        S.emit("pool", lambda e: e.tensor_copy(out=pre[:, :, 0:3], in_=pre[:, :, 512:515]), reads=["pre"], writes=["pre"])
        S.emit("act", lambda e: e.activation(out=cv[:, 0:2, :], in_=cv[:, 0:2, :], func=AF.Silu), reads=["cv"], writes=["cv"])
        S.emit("act", lambda e: e.activation(out=I["vT"], in_=cv[:, 2:4, :], func=AF.Silu), reads=["cv"], writes=["vT" + sfx])
        S.emit("pool", lambda e: e.tensor_tensor(out=sq, in0=cv[:, 0:2, :], in1=cv[:, 0:2, :], op=ALU.mult), reads=["cv"],
               writes=["sq"])
        yield
        for c2 in range(2):
            bank = 1 + c2
            pk = "pp%d" % bank
            S.emit("pe", lambda e, c2=c2, bank=bank: e.matmul(pf(bank, 0, 512), lhsT=ones, rhs=sq[:, c2, :], start=True, stop=True),
                   reads=["cst", "sq"], writes=[pk])
            S.emit("act", lambda e, c2=c2, bank=bank: e.activation(out=rn[:, c2, :], in_=pf(bank, 0, 512), func=AF.Ln, bias=misc[:, M_EPS:M_EPS + 1]),
                   reads=[pk], writes=["rn"])
        S.emit("act", lambda e: e.activation(out=rn, in_=rn, func=AF.Exp, scale=-0.5), reads=["rn"], writes=["rn"])
        S.emit("dve", lambda e: e.scalar_tensor_tensor(out=I["qTn"], in0=cv[:, 0, :], scalar=float(128.0 ** -0.5), in1=rn[:, 0, :],
                                                       op0=ALU.mult, op1=ALU.mult), reads=["cv", "rn"], writes=["qTn" + sfx])
        S.emit("dve", lambda e: e.tensor_tensor(out=I["kTn"], in0=cv[:, 1, :], in1=rn[:, 1, :], op=ALU.mult), reads=["cv", "rn"],
               writes=["kTn" + sfx])
        yield
        if LEVEL < 4:
            return
        S.emit("dve", lambda e: e.tensor_scalar(out=pos4, in0=misc[:, M_POS4:M_POS4 + 4], scalar1=float(bstart), scalar2=None,
                                                op0=ALU.add), reads=["cst"], writes=["pos4"])
        for t in range(4):
            S.emit("dve", lambda e, t=t: e.tensor_scalar(out=ang[:, t, 0:64], in0=invf, scalar1=pos4[:, t:t + 1], scalar2=None,
                                                         op0=ALU.mult), reads=["cst", "pos4"], writes=["ang"])
        S.emit("dve", lambda e: e.tensor_scalar(out=ang[:, :, 64:128], in0=ang[:, :, 0:64], scalar1=float(np.pi / 2), scalar2=None,
                                                op0=ALU.add), reads=["ang"], writes=["ang"])
        S.emit("dve", lambda e: e.tensor_scalar(out=ang2, in0=ang, scalar1=float(1.0 / TWO_PI), scalar2=MAGIC, op0=ALU.mult,
                                                op1=ALU.add), reads=["ang"], writes=["ang2"])
        S.emit("dve", lambda e: e.tensor_scalar(out=ang2, in0=ang2, scalar1=-MAGIC, scalar2=None, op0=ALU.add), reads=["ang2"],
               writes=["ang2"])
        S.emit("dve", lambda e: e.scalar_tensor_tensor(out=ang, in0=ang2, scalar=-CW1, in1=ang, op0=ALU.mult, op1=ALU.add),
               reads=["ang", "ang2"], writes=["ang"])
        S.emit("dve", lambda e: e.scalar_tensor_tensor(out=ang, in0=ang2, scalar=-CW2, in1=ang, op0=ALU.mult, op1=ALU.add),
               reads=["ang", "ang2"], writes=["ang"])
        S.emit("dve", lambda e: e.tensor_scalar(out=ang, in0=ang, scalar1=-3.1415925, scalar2=3.1415925, op0=ALU.max, op1=ALU.min),
               reads=["ang"], writes=["ang"])
        S.emit("act", lambda e: e.activation(out=SC, in_=ang, func=AF.Sin), reads=["ang"], writes=["SC"])
        yield
        if LEVEL < 5:
            return
        for t in range(4):
            ut = lambda kc, t=t: uT[:, kc, t * 128:(t + 1) * 128]
            for grp in range(3):
                bank = 1 + ((t * 3 + grp) % 2)
                pk = "pp%d" % bank
                for kc in range(16):
                    S.emit("pe", lambda e, kc=kc, grp=grp, bank=bank, ut=ut: e.matmul(pf(bank, 0, 512), lhsT=ut(kc),
                                                                                       rhs=Wt[:, kc, grp * 512:(grp + 1) * 512],
                                                                                       start=(kc == 0), stop=(kc == 15)),
                           reads=["Wt", "uT"], writes=[pk], inc=(kc == 15))
                if SUB < 2:
                    pass
                elif grp in (0, 2):
                    if SUB < 3 or SUB in (21, 22):
                        continue
                    G = I["Ga"] if grp == 0 else I["Gb"]
                    gk = ("Ga" if grp == 0 else "Gb") + sfx
                    nw = gnw if grp == 0 else rnw
                    S.emit("act", lambda e, bank=bank: e.activation(out=szt, in_=pf(bank, 0, 256), func=AF.Silu), reads=[pk],
                           writes=["szt"])
                    S.emit("act", lambda e, bank=bank: e.activation(out=tgt, in_=pf(bank, 256, 256), func=AF.Tanh, scale=0.5),
                           reads=[pk], writes=["tgt"])
                    if SUB < 4:
                        continue
                    S.emit("dve", lambda e: e.tensor_scalar(out=tgt, in0=tgt, scalar1=0.5, scalar2=0.5, op0=ALU.mult, op1=ALU.add),
                           reads=["tgt"], writes=["tgt"])
                    S.emit("pool", lambda e, nw=nw: e.tensor_tensor(out=szt, in0=szt, in1=nw, op=ALU.mult), reads=["szt", "cst"],
                           writes=["szt"])
                    S.emit("pool", lambda e, G=G, t=t: e.tensor_tensor(out=G[:, t, :], in0=szt, in1=tgt, op=ALU.mult),
                           reads=["szt", "tgt"], writes=[gk])
                else:
                  if SUB != 21:
                    S.emit("act", lambda e, bank=bank, t=t: e.activation(out=qk_sb[:, 0, t, :], in_=pf(bank, 0, 128), func=AF.Copy,
                                                                         scale=cols[:, XI:XI + 1]), reads=[pk, "cols"],
                           writes=["qk_sb"])
                    S.emit("act", lambda e, bank=bank, t=t: e.activation(out=qk_sb[:, 1, t, :], in_=pf(bank, 128, 128), func=AF.Copy,
                                                                         scale=cols[:, XIK:XIK + 1]), reads=[pk, "cols"],
                           writes=["qk_sb"])
                  if SUB != 22:
                    S.emit("dve", lambda e, bank=bank, t=t: e.tensor_copy(out=I["vb"][:, t, :], in_=pf(bank, 256, 256)), reads=[pk],
                           writes=["vb" + sfx])
                yield
            if LEVEL < 6:
                continue
            bankb = 1 + (t % 2)
            pba = "pp%d" % bankb
            for kc in range(16):
                S.emit("pe", lambda e, kc=kc, ut=ut, bankb=bankb: e.matmul(pf(bankb, 0, 2), lhsT=ut(kc), rhs=Wba[:, kc, :], start=(kc == 0),
                                                              stop=(kc == 15)), reads=["Wba", "uT"], writes=[pba], inc=(kc == 15))
            S.emit("act", lambda e, bankb=bankb: e.activation(out=bat[:, 0:1], in_=pf(bankb, 0, 1), func=AF.Tanh, scale=0.5), reads=[pba],
                   writes=["bat"])
            S.emit("dve", lambda e, t=t: e.tensor_scalar(out=I["bcol"][:, t:t + 1], in0=bat[:, 0:1], scalar1=0.5, scalar2=0.5,
                                                         op0=ALU.mult, op1=ALU.add), reads=["bat"], writes=["bcol" + sfx])
            S.emit("dve", lambda e, t=t: e.tensor_scalar(out=I["nbcol"][:, t:t + 1], in0=bat[:, 0:1], scalar1=-0.5, scalar2=-0.5,
                                                         op0=ALU.mult, op1=ALU.add), reads=["bat"], writes=["nbcol" + sfx])
            S.emit("act", lambda e, bankb=bankb: e.activation(out=bat[:, 1:2], in_=pf(bankb, 1, 1), func=AF.Exp, bias=cols[:, DTB:DTB + 1]),
                   reads=[pba, "cols", "bat"], writes=["bat"])
            S.emit("act", lambda e: e.activation(out=bat[:, 2:3], in_=bat[:, 1:2], func=AF.Ln, bias=misc[:, M_ONE:M_ONE + 1]), reads=["bat"],
                   writes=["bat"])
            S.emit("dve", lambda e, t=t: e.tensor_scalar(out=I["gcol"][:, t:t + 1], in0=bat[:, 2:3], scalar1=cols[:, NEGA:NEGA + 1],
                                                         scalar2=None, op0=ALU.mult), reads=["bat", "cols"], writes=["gcol" + sfx])
            yield
        if LEVEL < 7:
            return
        for w in range(2):
            src = qk_sb[:, w, :, :]
            dst = I["qx"] if w == 0 else I["kz"]
            dk_ = ("qx" if w == 0 else "kz") + sfx
            t1 = rt1[:, w, :, :]
            t2 = rt2[:, w, :, :]
            sin = SC[:, :, 0:64]
            cos = SC[:, :, 64:128]
            S.emit("dve", lambda e, src=src, t1=t1, cos=cos: e.tensor_tensor(out=t1[:, :, 0:64], in0=src[:, :, 0:64], in1=cos,
                                                                              op=ALU.mult), reads=["qk_sb", "SC"], writes=["rt1"])
            S.emit("dve", lambda e, src=src, t1=t1, cos=cos: e.tensor_tensor(out=t1[:, :, 64:128], in0=src[:, :, 64:128], in1=cos,
                                                                              op=ALU.mult), reads=["qk_sb", "SC"], writes=["rt1"])
            S.emit("pool", lambda e, src=src, t2=t2, sin=sin: e.tensor_tensor(out=t2[:, :, 0:64], in0=src[:, :, 64:128], in1=sin,
                                                                               op=ALU.mult), reads=["qk_sb", "SC"], writes=["rt2"])
            S.emit("pool", lambda e, src=src, t2=t2, sin=sin: e.tensor_tensor(out=t2[:, :, 64:128], in0=src[:, :, 0:64], in1=sin,
                                                                               op=ALU.mult), reads=["qk_sb", "SC"], writes=["rt2"])
            S.emit("dve", lambda e, dst=dst, t1=t1, t2=t2: e.tensor_tensor(out=dst[:, :, 0:64], in0=t1[:, :, 0:64], in1=t2[:, :, 0:64],
                                                                            op=ALU.subtract), reads=["rt1", "rt2"], writes=[dk_])
            S.emit("dve", lambda e, dst=dst, t1=t1, t2=t2: e.tensor_tensor(out=dst[:, :, 64:128], in0=t1[:, :, 64:128],
                                                                            in1=t2[:, :, 64:128], op=ALU.add),
                   reads=["rt1", "rt2"], writes=[dk_])
        yield

    def stage2(b):
        p = b % 2
        I = inter[p]
        sfx = "_%d" % p
        tok0 = b * 512
        if b % BPB == 0:
            for nm, t in (("Sa", Sa), ("Sabf", Sabf), ("Zs", Zs), ("Zbf", Zbf)):
                S.emit("pool", lambda e, t=t: e.memset(t, 0.0), writes=[nm])
        for t in range(4):
            cs = slice(t * 128, (t + 1) * 128)
            kT = I["kTn"][:, cs]
            qT = I["qTn"][:, cs]
            gcol = I["gcol"][:, t:t + 1]
            bcol = I["bcol"][:, t:t + 1]
            nbcol = I["nbcol"][:, t:t + 1]
            rI = ["kTn" + sfx, "qTn" + sfx, "vT" + sfx, "gcol" + sfx, "bcol" + sfx, "nbcol" + sfx]
            S.emit("dve", lambda e, gcol=gcol: e.tensor_scalar(out=gB, in0=ones, scalar1=gcol, scalar2=None, op0=ALU.mult),
                   reads=["cst", "gcol" + sfx], writes=["gB"])
            S.emit("dve", lambda e, gcol=gcol: e.tensor_scalar(out=gsel, in0=misc[:, M_CSEL:M_CSEL + 2], scalar1=gcol, scalar2=None,
                                                               op0=ALU.mult), reads=["cst", "gcol" + sfx], writes=["gsel"])
            S.emit("pe", lambda e, gcol=gcol: e.matmul(pf(4, 384, 1), lhsT=tri, rhs=gcol, start=True, stop=True),
                   reads=["cst", "gcol" + sfx], writes=["pgs"], inc=False)
            S.emit("pe", lambda e, gcol=gcol: e.matmul(pf(4, 385, 1), lhsT=samec, rhs=gcol, start=True, stop=True),
                   reads=["cst", "gcol" + sfx], writes=["pgs"], inc=False)
            S.emit("pe", lambda e: e.matmul(pf(4, 386, 2), lhsT=ones, rhs=gsel, start=True, stop=True),
                   reads=["cst", "gsel"], writes=["pgs"])
            S.emit("pe", lambda e: e.matmul(pf(3, 0, 128), lhsT=gB, rhs=tri, start=True, stop=True), reads=["gB", "cst"],
                   writes=["pGCB"])
            S.emit("pe", lambda e, kT=kT: e.matmul(pf(3, 128, 128), lhsT=kT, rhs=kT, start=True, stop=True), reads=rI[0:1],
                   writes=["pKK"])
            S.emit("pe", lambda e, kT=kT, qT=qT: e.matmul(pf(3, 256, 128), lhsT=kT, rhs=qT, start=True, stop=True), reads=rI[0:2],
                   writes=["pQK"])
            S.emit("dve", lambda e: e.tensor_copy(out=gcs[:, 0:4], in_=pf(4, 384, 4)), reads=["pgs"], writes=["gcs"])
            S.emit("dve", lambda e: e.tensor_tensor(out=gcs[:, 6:7], in0=gcs[:, 1:2], in1=gcs[:, 0:1], op=ALU.subtract),
                   reads=["gcs"], writes=["gcs"])
            S.emit("act", lambda e: e.activation(out=gcs[:, 4:5], in_=gcs[:, 0:1], func=AF.Exp), reads=["gcs"], writes=["gcs"])
            S.emit("act", lambda e: e.activation(out=gcs[:, 2:4], in_=gcs[:, 2:4], func=AF.Exp), reads=["gcs"], writes=["gcs"])
            S.emit("act", lambda e: e.activation(out=gcs[:, 6:7], in_=gcs[:, 6:7], func=AF.Exp), reads=["gcs"], writes=["gcs"])
            S.emit("dve", lambda e: e.tensor_scalar(out=gcs[:, 5:6], in0=gcs[:, 4:5], scalar1=-1.0, scalar2=None, op0=ALU.mult),
                   reads=["gcs"], writes=["gcs"])
            S.emit("dve", lambda e: e.scalar_tensor_tensor(out=tmpD, in0=pf(3, 0, 128), scalar=gcs[:, 0:1], in1=negm,
                                                           op0=ALU.subtract, op1=ALU.add), reads=["pGCB", "gcs", "cst"],
                   writes=["tmpD"])
            S.emit("act", lambda e: e.activation(out=Ef, in_=tmpD, func=AF.Exp), reads=["tmpD"], writes=["Ef"])
            S.emit("act", lambda e: e.activation(out=EGCB, in_=pf(3, 0, 128), func=AF.Exp), reads=["pGCB"], writes=["EGCB"])
            S.emit("pool", lambda e: e.tensor_tensor(out=Es, in0=Ef, in1=strict, op=ALU.mult), reads=["Ef", "cst"], writes=["Es"])
            S.emit("dve", lambda e, nbcol=nbcol: e.scalar_tensor_tensor(out=Nn[0], in0=pf(3, 128, 128), scalar=nbcol, in1=Es,
                                                                        op0=ALU.mult, op1=ALU.mult),
                   reads=["pKK", "nbcol" + sfx, "Es"], writes=["N0"])
            S.emit("dve", lambda e: e.tensor_tensor(out=attnT, in0=pf(3, 256, 128), in1=Ef, op=ALU.mult), reads=["pQK", "Ef"],
                   writes=["attnT"])
            S.emit("pe", lambda e: e.transpose(out=pf(4, 0, 128), in_=Nn[0], identity=ident), reads=["N0", "cst"], writes=["pN"])
            S.emit("act", lambda e: e.activation(out=NT[0], in_=pf(4, 0, 128), func=AF.Copy), reads=["pN"], writes=["NT0"])
            S.emit("pool", lambda e: e.tensor_tensor(out=Xm, in0=Nn[0], in1=ident, op=ALU.add), reads=["N0", "cst"], writes=["Xm"])
            yield
            for r in range(5):
                a, bb = r % 2, (r + 1) % 2
                last = (r == 4)
                if not last:
                    S.emit("pe", lambda e, a=a: e.matmul(pf(4, 0, 128), lhsT=NT[a], rhs=Nn[a], start=True, stop=True),
                           reads=["NT%d" % a, "N%d" % a], writes=["pN"])
                S.emit("pe", lambda e, a=a: e.matmul(pf(4, 128, 128), lhsT=Nn[a], rhs=NT[a], start=True, stop=True),
                       reads=["NT%d" % a, "N%d" % a], writes=["pNT"])
                if not last:
                    S.emit("act", lambda e, bb=bb: e.activation(out=Nn[bb], in_=pf(4, 0, 128), func=AF.Copy), reads=["pN"],
                           writes=["N%d" % bb])
                S.emit("dve", lambda e, bb=bb: e.tensor_copy(out=NT[bb], in_=pf(4, 128, 128)), reads=["pNT"], writes=["NT%d" % bb])
                S.emit("pe", lambda e, bb=bb: e.matmul(pf(4, 256, 128), lhsT=NT[bb], rhs=Xm, start=True, stop=True),
                       reads=["NT%d" % bb, "Xm"], writes=["pX"])
                if not last:
                    S.emit("dve", lambda e: e.tensor_tensor(out=Xm, in0=pf(4, 256, 128), in1=Xm, op=ALU.add), reads=["pX", "Xm"],
                           writes=["Xm"])
                else:
                    S.emit("dve", lambda e: e.tensor_tensor(out=M2bf, in0=pf(4, 256, 128), in1=Xm, op=ALU.add), reads=["pX", "Xm"],
                           writes=["M2bf"])
                yield
            S.emit("pe", lambda e, kT=kT: e.transpose(out=pb16(5, 0, 128), in_=kT, identity=identb), reads=rI[0:1] + ["identb"],
                   writes=["ptA"])
            for c in range(2):
                rows = slice(c * 64, (c + 1) * 64)
                S.emit("act", lambda e, c=c, rows=rows: e.activation(out=kdc[c][rows, :], in_=pb16(5, 0, 128)[rows, :], func=AF.Copy,
                                                                     scale=gcs[rows, 6:7]), reads=["ptA", "gcs"], writes=["kdc%d" % c])
            for h in range(2):
                S.emit("pe", lambda e, h=h, cs=cs: e.transpose(out=pb16(5, 512 + h * 128, 128), in_=I["vT"][:, h, cs], identity=identb),
                       reads=["vT" + sfx, "identb"], writes=["ptB"], inc=(h == 1))
            S.emit("dve", lambda e: e.tensor_copy(out=vtok, in_=pb16(5, 512, 256)), reads=["ptB"], writes=["vtok"])
            for c in range(2):
                ccs = slice(c * 64, (c + 1) * 64)
                S.emit("pool", lambda e, c=c, ccs=ccs, kT=kT: e.tensor_copy(out=kTc[c][:, ccs], in_=kT[:, ccs]), reads=rI[0:1],
                       writes=["kTc%d" % c])
                S.emit("dve", lambda e, c=c, ccs=ccs, qT=qT: e.tensor_tensor(out=qgTc[c][:, ccs], in0=qT[:, ccs], in1=EGCB[:, ccs],
                                                                            op=ALU.mult), reads=rI[1:2] + ["EGCB"],
                       writes=["qgTc%d" % c])
            yield
            S.emit("pe", lambda e, t=t: e.transpose(out=pb16(6, 0, 128), in_=I["qx"][:, t, :], identity=identb),
                   reads=["qx" + sfx, "identb"], writes=["ptD"])
            S.emit("pe", lambda e, t=t: e.transpose(out=pb16(6, 512, 128), in_=I["kz"][:, t, :], identity=identb),
                   reads=["kz" + sfx, "identb"], writes=["ptC"])
            S.emit("act", lambda e: e.activation(out=qxT, in_=pb16(6, 0, 128), func=AF.Copy), reads=["ptD"], writes=["qxT"])
            S.emit("dve", lambda e: e.tensor_copy(out=kzT, in_=pb16(6, 512, 128)), reads=["ptC"], writes=["kzT"])
            for c in range(2):
                ccs = slice(c * 64, (c + 1) * 64)
                S.emit("pool", lambda e, c=c, ccs=ccs: e.tensor_copy(out=qxTc[c][:, ccs], in_=qxT[:, ccs]), reads=["qxT"],
                       writes=["qxTc%d" % c])
                S.emit("pool", lambda e, c=c, ccs=ccs, t=t: e.tensor_copy(out=kzc[c][ccs, :], in_=I["kz"][ccs, t, :]),
                       reads=["kz" + sfx], writes=["kzc%d" % c])
            S.emit("pe", lambda e: e.matmul(pf(3, 384, 128), lhsT=kzT, rhs=qxT, start=True, stop=True), reads=["kzT", "qxT"],
                   writes=["pSC"])
            S.emit("dve", lambda e: e.tensor_tensor(out=scT, in0=pf(3, 384, 128), in1=mask01, op=ALU.mult), reads=["pSC", "cst"],
                   writes=["scT"])
            yield
            vbt = I["vb"][:, t, :]
            for c in range(2):
                rows = slice(c * 64, (c + 1) * 64)
                S.emit("pe", lambda e, c=c: e.matmul(pf(7, 0, 256), lhsT=qgTc[c], rhs=Sabf, start=(c == 0), stop=False),
                       reads=["qgTc%d" % c, "Sabf"], writes=["pOa"])
                S.emit("pe", lambda e, c=c: e.matmul(pf(7, 256, 256), lhsT=qxTc[c], rhs=Zbf, start=False, stop=False),
                       reads=["qxTc%d" % c, "Zbf"], writes=["pOb"])
                S.emit("pe", lambda e, c=c: e.matmul(pf(5, 0, 256), lhsT=kTc[c], rhs=Sabf, start=True, stop=True),
                       reads=["kTc%d" % c, "Sabf"], writes=["p1"])
                S.emit("dve", lambda e, rows=rows: e.scalar_tensor_tensor(out=Rbf[rows, :], in0=pf(5, 0, 256)[rows, :],
                                                                          scalar=gcs[rows, 5:6], in1=vtok[rows, :], op0=ALU.mult,
                                                                          op1=ALU.add), reads=["p1", "gcs", "vtok"], writes=["Rbf"])
                S.emit("pe", lambda e: e.matmul(pf(5, 256, 256), lhsT=M2bf, rhs=Rbf, start=True, stop=True), reads=["M2bf", "Rbf"],
                       writes=["p2"])
                S.emit("act", lambda e, rows=rows, bcol=bcol: e.activation(out=vnbf[rows, :], in_=pf(5, 256, 256)[rows, :],
                                                                            func=AF.Copy, scale=bcol[rows, :]),
                       reads=["p2", "bcol" + sfx], writes=["vnbf"])
                S.emit("pe", lambda e, c=c: e.matmul(pf(6, 0, 256), lhsT=kdc[c], rhs=vnbf, start=True, stop=True),
                       reads=["kdc%d" % c, "vnbf"], writes=["p5"])
                S.emit("dve", lambda e, c=c: e.scalar_tensor_tensor(out=Sa, in0=Sa, scalar=gcs[:, 2 + c:3 + c], in1=pf(6, 0, 256),
                                                                    op0=ALU.mult, op1=ALU.add), reads=["p5", "gcs", "Sa"],
                       writes=["Sa"])
                S.emit("act", lambda e: e.activation(out=Sabf, in_=Sa, func=AF.Copy), reads=["Sa"], writes=["Sabf"])
                S.emit("pe", lambda e, c=c, vbt=vbt: e.matmul(pf(6, 256, 256), lhsT=kzc[c], rhs=vbt, start=True, stop=True),
                       reads=["kzc%d" % c, "vb" + sfx], writes=["pU"])
                S.emit("dve", lambda e: e.scalar_tensor_tensor(out=Zs, in0=Zs, scalar=cols[:, GC:GC + 1], in1=pf(6, 256, 256),
                                                               op0=ALU.mult, op1=ALU.add), reads=["pU", "cols", "Zs"], writes=["Zs"])
                S.emit("pool", lambda e: e.tensor_scalar(out=Zbf, in0=Zs, scalar1=cols[:, GC:GC + 1], scalar2=None, op0=ALU.mult),
                       reads=["Zs", "cols"], writes=["Zbf"])
                yield
            S.emit("pe", lambda e, vbt=vbt: e.matmul(pf(7, 256, 256), lhsT=scT, rhs=vbt, start=False, stop=False),
                   reads=["scT", "vb" + sfx], writes=["pOb"])
            S.emit("pe", lambda e: e.matmul(pf(7, 0, 256), lhsT=attnT, rhs=vnbf, start=False, stop=True), reads=["attnT", "vnbf"],
                   writes=["pOa"])
            S.emit("act", lambda e: e.activation(out=junk, in_=pf(7, 0, 256), func=AF.Square, accum_out=ost[:, 0:1]),
                   reads=["pOa"], writes=["junk", "ost"])
            S.emit("act", lambda e: e.activation(out=junk, in_=pf(7, 256, 256), func=AF.Square, accum_out=ost[:, 1:2]),
                   reads=["pOb", "ost"], writes=["junk", "ost"])
            S.emit("dve", lambda e: e.tensor_reduce(out=ost[:, 2:3], in_=pf(7, 256, 256), axis=AX.X, op=ALU.add), reads=["pOb", "ost"],
                   writes=["ost"])
            S.emit("dve", lambda e: e.tensor_scalar(out=ost[:, 3:4], in0=ost[:, 2:3], scalar1=1.0 / 256, scalar2=None, op0=ALU.mult),
                   reads=["ost"], writes=["ost"])
            S.emit("dve", lambda e: e.tensor_scalar(out=ost[:, 4:5], in0=ost[:, 0:1], scalar1=1.0 / 256, scalar2=EPS, op0=ALU.mult,
                                                    op1=ALU.add), reads=["ost"], writes=["ost"])
            S.emit("dve", lambda e: e.tensor_tensor(out=ost[:, 6:7], in0=ost[:, 3:4], in1=ost[:, 3:4], op=ALU.mult), reads=["ost"],
                   writes=["ost"])
            S.emit("dve", lambda e: e.tensor_scalar(out=ost[:, 5:6], in0=ost[:, 1:2], scalar1=1.0 / 256, scalar2=EPS, op0=ALU.mult,
                                                    op1=ALU.add), reads=["ost"], writes=["ost"])
            S.emit("dve", lambda e: e.tensor_tensor(out=ost[:, 5:6], in0=ost[:, 5:6], in1=ost[:, 6:7], op=ALU.subtract),
                   reads=["ost"], writes=["ost"])
            S.emit("act", lambda e: e.activation(out=ost[:, 8:10], in_=ost[:, 4:6], func=AF.Ln), reads=["ost"], writes=["ost"])
            S.emit("act", lambda e: e.activation(out=ost[:, 8:10], in_=ost[:, 8:10], func=AF.Exp, scale=-0.5), reads=["ost"],
                   writes=["ost"])
            S.emit("dve", lambda e: e.scalar_tensor_tensor(out=ost[:, 10:11], in0=ost[:, 3:4], scalar=-1.0, in1=ost[:, 9:10],
                                                           op0=ALU.mult, op1=ALU.mult), reads=["ost"], writes=["ost"])
            S.emit("act", lambda e: e.activation(out=ta, in_=pf(7, 0, 256), func=AF.Copy, scale=ost[:, 8:9]), reads=["pOa", "ost"],
                   writes=["ta"])
            S.emit("act", lambda e: e.activation(out=tb, in_=pf(7, 256, 256), func=AF.Identity, scale=ost[:, 9:10],
                                                 bias=ost[:, 10:11]), reads=["pOb", "ost"], writes=["tb"])
            S.emit("pool", lambda e, t=t: e.tensor_tensor(out=ta, in0=ta, in1=I["Ga"][:, t, :], op=ALU.mult),
                   reads=["ta", "Ga" + sfx], writes=["ta"])
            S.emit("pool", lambda e, t=t: e.tensor_tensor(out=tb, in0=tb, in1=I["Gb"][:, t, :], op=ALU.mult),
                   reads=["tb", "Gb" + sfx], writes=["tb"])
            S.emit("dve", lambda e: e.tensor_tensor(out=mixbf, in0=ta, in1=tb, op=ALU.add), reads=["ta", "tb"], writes=["mixbf"])
            for h in range(2):
                S.emit("pe", lambda e, h=h: e.transpose(out=pb16(5, 512 + h * 128, 128), in_=mixbf[:, h * 128:(h + 1) * 128],
                                                         identity=identb), reads=["mixbf", "identb"], writes=["ptB"], inc=(h == 1))
            S.emit("act", lambda e, t=t: e.activation(out=mTs[:, :, t * 128:(t + 1) * 128],
                                                      in_=pb16(5, 512, 256).rearrange("p (a b) -> p a b", a=2), func=AF.Copy),
                   reads=["ptB"], writes=["mTs"])
            yield
        dst = ag_in.rearrange("(h p) n -> p h n", p=128)[:, :, tok0:tok0 + 512]
        dma("sp", dst, mTs, sl["mts"], reads=["mTs"], writes=["ag_in"])
        if debug:
            dstd = dbg_mixed.rearrange("(h p) n -> p h n", p=128)[:, :, tok0:tok0 + 512]
            dma("sp", dstd, mTs, sl["dbg"], reads=["mTs"])
        yield

    def drain(g):
        for _ in g:
            pass

    g1 = stage1(0)
    drain(g1)
    for b in range(NBLK if NBLK_LIMIT is None else NBLK_LIMIT):
        g2 = stage2(b)
        g1 = stage1(b + 1) if b + 1 < NBLK else iter(())
        a_done = b_done = False
        while not (a_done and b_done):
            if not b_done:
                try:
                    next(g2)
                except StopIteration:
                    b_done = True
            if not a_done:
                try:
                    next(g1)
                except StopIteration:
                    a_done = True

    S.barrier()
    if not NOCC:
        S.emit("pool", lambda e: e.collective_compute("AllGather", ALU.bypass, replica_groups=[list(range(NCORES))],
                                                      ins=[ag_in.opt()], outs=[ag_out.opt()]),
               reads=["ag_in"], writes=["ag_out"], slot=s_cc)
    S.barrier()

    A.off = 0
    TP = min(512, TPC)
    NPASS = TPC // TP
    NT_ = TP // 128
    NTB = max(1, TP // 512)
    TBW = min(512, TP)
    cstB = A.take([NCST], F32)
    identbB = A.take([128], BF16)
    n2 = A.take([16], F32)
    nfw = A.take([D], F32)
    hacc = A.take([NT_, D], F32)
    hnT = A.take([16, TP], BF16)
    mT = A.take([16, TP], BF16)
    wo = [A.take([16, 512], BF16) for _ in range(2)]
    xnB = A.take([D], BF16)
    stB = A.take([16], F32)
    aT = [A.take([2, TP], BF16) for _ in range(2)]
    sg = A.take([512], F32)
    WG = [A.take([16, 256], BF16) for _ in range(2)]
    WU = [A.take([16, 256], BF16) for _ in range(2)]
    WD = [A.take([2, D], BF16) for _ in range(2)]
    ot = A.take([D], F32)
    print("phase B arena bytes", A.off)
    slB = {n: S.slot(n) for n in ("n2", "nf", "hacc", "mT", "wo0", "wo1", "wg0", "wg1", "wu0", "wu1", "wd0", "wd1", "out")}
    identB_ = identbB
    dma("sp", n2, n2_d, slB["n2"], writes=["n2"])
    dma("sp", nfw, nf_d, slB["nf"], writes=["nfw"])
    wo_v = wo_d.rearrange("(k p) n -> p k n", p=128)
    wg_v = wg_d.rearrange("(k p) n -> p k n", p=128)
    wu_v = wu_d.rearrange("(k p) n -> p k n", p=128)
    wd_v = wd_d.rearrange("(k p) n -> p k n", p=128)
    ag_v = ag_out.rearrange("(k p) n -> p k n", p=128)
    core_tok0 = None
    NDB = DFF // 256
    out_toks = []

    for ps in range(0 if STOPA else NPASS):
        t0 = ps * TP
        dma("sp", hacc, xb[t0:t0 + TP, :].rearrange("(t p) n -> p t n", p=128), slB["hacc"], writes=["hacc"])
        S.emit("sp", lambda e, t0=t0: e.dma_start(out=mT, in_=(ag_v[:, :, bass.ds(e.partition_id() * TPC + t0, TP)] if not STATIC_TOK else ag_v[:, :, t0:t0 + TP])),
               reads=["ag_out"], writes=["mT"], slot=slB["mT"])
        for cb in range(4):
            wb_ = wo[cb % 2]
            wk = "wo%d" % (cb % 2)
            dma("pool", wb_, wo_v[:, :, cb * 512:(cb + 1) * 512], slB[wk], writes=[wk])
            for t in range(NT_):
                bank = 1 + (t % 6)
                pk = "pp%d" % bank
                for kc in range(16):
                    S.emit("pe", lambda e, kc=kc, t=t, bank=bank, wb_=wb_: e.matmul(pf(bank, 0, 512), lhsT=mT[:, kc, t * 128:(t + 1) * 128],
                                                                                    rhs=wb_[:, kc, :], start=(kc == 0), stop=(kc == 15)),
                           reads=["mT", wk], writes=[pk], inc=(kc == 15))
                S.emit("dve", lambda e, t=t, cb=cb, bank=bank: e.tensor_tensor(out=hacc[:, t, cb * 512:(cb + 1) * 512], in0=pf(bank, 0, 512),
                                                                              in1=hacc[:, t, cb * 512:(cb + 1) * 512], op=ALU.add),
                       reads=[pk, "hacc"], writes=["hacc"])
        for t in range(NT_):
            S.emit("act", lambda e, t=t: e.activation(out=xnB, in_=hacc[:, t, :], func=AF.Square, accum_out=stB[:, 0:1]),
                   reads=["hacc"], writes=["xnB", "stB"])
            S.emit("dve", lambda e: e.tensor_scalar(out=stB[:, 1:2], in0=stB[:, 0:1], scalar1=1.0 / D, scalar2=EPS, op0=ALU.mult,
                                                    op1=ALU.add), reads=["stB"], writes=["stB"])
            S.emit("act", lambda e: e.activation(out=stB[:, 1:2], in_=stB[:, 1:2], func=AF.Ln), reads=["stB"], writes=["stB"])
            S.emit("act", lambda e: e.activation(out=stB[:, 1:2], in_=stB[:, 1:2], func=AF.Exp, scale=-0.5), reads=["stB"],
                   writes=["stB"])
            S.emit("act", lambda e, t=t: e.activation(out=xnB, in_=hacc[:, t, :], func=AF.Copy, scale=stB[:, 1:2]),
                   reads=["hacc", "stB"], writes=["xnB"])
            for r in range(4):
                for q in range(4):
                    kc = r * 4 + q
                    S.emit("pe", lambda e, kc=kc, q=q: e.transpose(out=pb16(0, q * 128, 128), in_=xnB[:, kc * 128:(kc + 1) * 128],
                                                                    identity=identB_), reads=["xnB", "identbB"], writes=["pt"],
                           inc=(q == 3))
                for q in range(4):
                    kc = r * 4 + q
                    if q % 2 == 0:
                        S.emit("dve", lambda e, kc=kc, q=q, t=t: e.tensor_scalar(out=hnT[:, kc, t * 128:(t + 1) * 128],
                                                                               in0=pb16(0, q * 128, 128), scalar1=n2[:, kc:kc + 1],
                                                                               scalar2=None, op0=ALU.mult), reads=["pt", "n2"],
                               writes=["hnT"])
                    else:
                        S.emit("act", lambda e, kc=kc, q=q, t=t: e.activation(out=hnT[:, kc, t * 128:(t + 1) * 128],
                                                                            in_=pb16(0, q * 128, 128), func=AF.Copy,
                                                                            scale=n2[:, kc:kc + 1]), reads=["pt", "n2"], writes=["hnT"])
        def load_w(db):
            i = db % 2
            dma("pool", WG[i], wg_v[:, :, db * 256:(db + 1) * 256], slB["wg%d" % i], writes=["WG%d" % i])
            dma("pool", WU[i], wu_v[:, :, db * 256:(db + 1) * 256], slB["wu%d" % i], writes=["WU%d" % i])
            dma("pool", WD[i], wd_v[:, db * 2:db * 2 + 2, :], slB["wd%d" % i], writes=["WD%d" % i])

        load_w(0)
        bk = 0
        for db in range(NDB):
            i = db % 2
            if db + 1 < NDB:
                load_w(db + 1)
            for c in range(2):
                for tb_ in range(NTB):
                    tsl = slice(tb_ * TBW, (tb_ + 1) * TBW)
                    bg = 1 + (bk % 7)
                    bu = 1 + ((bk + 1) % 7)
                    bk += 2
                    for kc in range(16):
                        S.emit("pe", lambda e, kc=kc, c=c, i=i, tsl=tsl, bg=bg: e.matmul(pf(bg, 0, TBW), lhsT=WG[i][:, kc, c * 128:(c + 1) * 128],
                                                                                         rhs=hnT[:, kc, tsl], start=(kc == 0), stop=(kc == 15)),
                               reads=["WG%d" % i, "hnT"], writes=["pp%d" % bg], inc=(kc == 15))
                    for kc in range(16):
                        S.emit("pe", lambda e, kc=kc, c=c, i=i, tsl=tsl, bu=bu: e.matmul(pf(bu, 0, TBW), lhsT=WU[i][:, kc, c * 128:(c + 1) * 128],
                                                                                         rhs=hnT[:, kc, tsl], start=(kc == 0), stop=(kc == 15)),
                               reads=["WU%d" % i, "hnT"], writes=["pp%d" % bu], inc=(kc == 15))
                    S.emit("act", lambda e, bg=bg: e.activation(out=sg[:, 0:TBW], in_=pf(bg, 0, TBW), func=AF.Silu), reads=["pp%d" % bg],
                           writes=["sg"])
                    S.emit("dve", lambda e, bu=bu, c=c, i=i, tsl=tsl: e.tensor_tensor(out=aT[i][:, c, tsl], in0=pf(bu, 0, TBW), in1=sg[:, 0:TBW],
                                                                                     op=ALU.mult), reads=["pp%d" % bu, "sg"],
                           writes=["aT%d" % i])
            for t in range(NT_):
                for cb in range(4):
                    bd = 1 + (bk % 7)
                    bk += 1
                    for c in range(2):
                        S.emit("pe", lambda e, c=c, i=i, t=t, cb=cb, bd=bd: e.matmul(pf(bd, 0, 512), lhsT=aT[i][:, c, t * 128:(t + 1) * 128],
                                                                                     rhs=WD[i][:, c, cb * 512:(cb + 1) * 512], start=(c == 0),
                                                                                     stop=(c == 1)), reads=["aT%d" % i, "WD%d" % i],
                               writes=["pp%d" % bd], inc=(c == 1))
                    eng = "dve" if (t * 4 + cb) % 3 != 2 else "pool"
                    if eng == "dve":
                        S.emit("dve", lambda e, t=t, cb=cb, bd=bd: e.tensor_tensor(out=hacc[:, t, cb * 512:(cb + 1) * 512], in0=pf(bd, 0, 512),
                                                                                  in1=hacc[:, t, cb * 512:(cb + 1) * 512], op=ALU.add),
                               reads=["pp%d" % bd, "hacc"], writes=["hacc"])
                    else:
                        S.emit("dve", lambda e, t=t, cb=cb, bd=bd: e.tensor_tensor(out=hacc[:, t, cb * 512:(cb + 1) * 512], in0=pf(bd, 0, 512),
                                                                                  in1=hacc[:, t, cb * 512:(cb + 1) * 512], op=ALU.add),
                               reads=["pp%d" % bd, "hacc"], writes=["hacc"])
        for t in range(NT_):
            S.emit("act", lambda e, t=t: e.activation(out=xnB, in_=hacc[:, t, :], func=AF.Square, accum_out=stB[:, 2:3]),
                   reads=["hacc"], writes=["xnB", "stB"])
            S.emit("dve", lambda e: e.tensor_scalar(out=stB[:, 3:4], in0=stB[:, 2:3], scalar1=1.0 / D, scalar2=EPS, op0=ALU.mult,
                                                    op1=ALU.add), reads=["stB"], writes=["stB"])
            S.emit("act", lambda e: e.activation(out=stB[:, 3:4], in_=stB[:, 3:4], func=AF.Ln), reads=["stB"], writes=["stB"])
            S.emit("act", lambda e: e.activation(out=stB[:, 3:4], in_=stB[:, 3:4], func=AF.Exp, scale=-0.5), reads=["stB"],
                   writes=["stB"])
            S.emit("dve", lambda e, t=t: e.scalar_tensor_tensor(out=ot, in0=hacc[:, t, :], scalar=stB[:, 3:4], in1=nfw, op0=ALU.mult,
                                                                op1=ALU.mult), reads=["hacc", "stB", "nfw"], writes=["ot"])
            tok = dma("sp", out_d[t0 + t * 128:t0 + (t + 1) * 128, :], ot, slB["out"], reads=["ot"])
            out_toks.append(tok)
    S.final_wait("sp", out_toks + [(sl["mts"].key, sl["mts"].count), (sl["dbg"].key, sl["dbg"].count)])
    S.replay()
    return nc


def _const_table(h, a_log, dt_bias, gdn_norm_w, ret_norm_w):
    c = np.zeros((128, NCST), np.float32)
    p = np.arange(128)
    ch = p // 64
    same = ch[:, None] == ch[None, :]
    c[:, C_ID:C_ID + 128] = np.eye(128, dtype=np.float32)
    c[:, C_TRI:C_TRI + 128] = (same & (p[:, None] <= p[None, :])).astype(np.float32)
    valid = same & (p[None, :] >= p[:, None])
    c[:, C_NEGM:C_NEGM + 128] = np.where(valid, 0.0, -30000.0).astype(np.float32)
    c[:, C_STRICT:C_STRICT + 128] = (same & (p[None, :] > p[:, None])).astype(np.float32)
    c[:, C_MASK:C_MASK + 128] = valid.astype(np.float32)
    c[:, C_SAMEC:C_SAMEC + 128] = same.astype(np.float32)
    c[:, C_ONES:C_ONES + 128] = 1.0
    inv = (np.float32(10000.0) ** (-np.arange(0, 128, 2, dtype=np.float32) / np.float32(128))).astype(np.float32)
    c[:, C_INVF:C_INVF + 64] = inv[None, :]
    m = C_MISC
    c[:, m + M_POSC] = (p % 64) + 1
    c[:, m + M_ALOG] = a_log[h]
    c[:, m + M_DTB] = dt_bias[h]
    c[:, m + M_LG] = np.log1p(-np.exp2(np.float32(-5.0 - h))).astype(np.float32)
    for t in range(4):
        c[:, m + M_POS4 + t] = p + 128 * t
    c[:, m + M_CSEL + 0] = (ch == 0)
    c[:, m + M_CSEL + 1] = (ch == 1)
    c[:, m + M_EPS] = EPS
    c[:, m + M_ONE] = 1.0
    c[:, C_GNW:C_GNW + 256] = gdn_norm_w[None, :]
    c[:, C_RNW:C_RNW + 256] = ret_norm_w[h * 256:(h + 1) * 256][None, :]
    return c


_NC_CACHE = {}


def make_in_maps(x, norm1_w, w_in, conv_w, a_log, dt_bias, gdn_norm_w, ret_norm_w, w_out, norm2_w, w_gate, w_up, w_down,
                 norm_f_w):
    f = np.float32
    x = np.asarray(x, f)
    B, SEQ, _ = x.shape
    NTOK = B * SEQ
    TPC = NTOK // NCORES
    xf = np.ascontiguousarray(x.reshape(NTOK, D))
    w_in0 = np.asarray(w_in, f)[0]
    conv0 = np.asarray(conv_w, f)[0]
    a_log0 = np.asarray(a_log, f)[0]
    dtb0 = np.asarray(dt_bias, f)[0]
    gnw0 = np.asarray(gdn_norm_w, f)[0]
    rnw0 = np.asarray(ret_norm_w, f)[0]
    n1 = np.ascontiguousarray(np.asarray(norm1_w, f)[0].reshape(16, 128).T)
    n2 = np.ascontiguousarray(np.asarray(norm2_w, f)[0].reshape(16, 128).T)
    nf = np.ascontiguousarray(np.tile(np.asarray(norm_f_w, f)[None, :], (128, 1)))
    wo = np.ascontiguousarray(np.asarray(w_out, f)[0])
    wg = np.ascontiguousarray(np.asarray(w_gate, f)[0])
    wu = np.ascontiguousarray(np.asarray(w_up, f)[0])
    wd = np.ascontiguousarray(np.asarray(w_down, f)[0])
    O_Z, O_B, O_A, O_RQ, O_RK, O_RV, O_RG, O_GA, O_GB = 4096, 6144, 6152, 6160, 7184, 8208, 10256, 12304, 14352
    maps = []
    for h in range(NCORES):
        qc = slice(h * 128, (h + 1) * 128)
        kc = slice(1024 + h * 128, 1024 + (h + 1) * 128)
        vc = slice(2048 + h * 256, 2048 + (h + 1) * 256)
        wq = np.ascontiguousarray(np.concatenate([w_in0[:, qc], w_in0[:, kc], w_in0[:, vc]], axis=1))
        cwh = np.concatenate([conv0[:, qc], conv0[:, kc], conv0[:, vc]], axis=1)
        cw = np.ascontiguousarray(cwh.reshape(4, 4, 128).transpose(2, 1, 0).reshape(128, 16))
        s256 = lambda o: slice(o + h * 256, o + (h + 1) * 256)
        s128 = lambda o: slice(o + h * 128, o + (h + 1) * 128)
        wt = np.ascontiguousarray(np.concatenate([
            w_in0[:, s256(O_Z)], w_in0[:, s256(O_GA)],
            w_in0[:, s128(O_RQ)], w_in0[:, s128(O_RK)], w_in0[:, s256(O_RV)],
            w_in0[:, s256(O_RG)], w_in0[:, s256(O_GB)]], axis=1))
        wba = np.ascontiguousarray(np.stack([w_in0[:, O_B + h], w_in0[:, O_A + h]], axis=1))
        maps.append({
            "x": xf, "xb": np.ascontiguousarray(xf[h * TPC:(h + 1) * TPC]),
            "wq": wq, "wt": wt, "wba": wba, "cw": cw, "n1": n1, "n2": n2, "nf": nf,
            "cst": _const_table(h, a_log0, dtb0, gnw0, rnw0),
            "w_out": wo, "w_gate": wg, "w_up": wu, "w_down": wd,
        })
    return maps, (B, SEQ)


def kernel(**inputs):
    maps, (B, SEQ) = make_in_maps(**inputs)
    key = SEQ
    if key not in _NC_CACHE:
        _NC_CACHE[key] = build(SEQ)
    nc = _NC_CACHE[key]
    res = run_bass_kernel_spmd(nc, maps, core_ids=list(range(NCORES)))
    out = np.concatenate([np.asarray(res.results[c]["out"], np.float32) for c in range(NCORES)], axis=0)
    return out.reshape(B, SEQ, D)
```

```python
import numpy as np
import concourse.bass as bass
import concourse.mybir as mybir
from concourse.bass_utils import run_bass_kernel_spmd

F32 = mybir.dt.float32
BF16 = mybir.dt.bfloat16
ALU = mybir.AluOpType
AF = mybir.ActivationFunctionType
AX = mybir.AxisListType

D = 2048
DFF = 5632
NCORES = 8
EPS = 1e-6
SAME_ENGINE_SYNC = True
PSUM_EXCL = True
STATIC_TOK = False
NOCC = False
STOPA = False
NBLK_LIMIT = None
LEVEL = 99
SUB = 99


class Slot:
    def __init__(self, sched, name, step=16):
        self.step = step
        self.key = "dma_" + name
        self.sem = sched.nc.alloc_semaphore("q_" + name)
        self.count = 0
        sched.semof[self.key] = self.sem


class Sched:
    ENG = ("pe", "dve", "act", "pool", "sp")

    def __init__(self, nc):
        self.nc = nc
        self.streams = {k: [] for k in self.ENG}
        self.semof = {k: nc.alloc_semaphore("c_" + k) for k in self.ENG}
        self.cnt = {k: 0 for k in self.ENG}
        self.waited = {k: {} for k in self.ENG}
        self.state = {}
        self.bank_of = {}
        self.bank_keys = {}

    def set_banks(self, mapping):
        for k, b in mapping.items():
            self.bank_of[k] = b
            self.bank_keys.setdefault(b, []).append(k)

    def slot(self, name, step=16):
        return Slot(self, name, step)

    def emit(self, eng, fn, reads=(), writes=(), slot=None, inc=True):
        bo = self.bank_of
        reads = [("B%d" % bo[r]) if r in bo else r for r in reads]
        writes = [("B%d" % bo[w]) if w in bo else w for w in writes]
        deps = []
        for r in reads:
            st = self.state.get(r)
            if st and st[0] is not None:
                deps.append(st[0])
            if st and PSUM_EXCL and r[0] == "B" and r[1:].isdigit():
                deps.extend(tk for tk in st[1] if tk[0] != eng)
        for w in writes:
            st = self.state.get(w)
            if st:
                if st[0] is not None:
                    deps.append(st[0])
                deps.extend(st[1])
        wd = self.waited[eng]
        m = {}
        for (k, v) in deps:
            if k == eng and (eng == "pe" or not SAME_ENGINE_SYNC):
                continue
            if wd.get(k, 0) >= v:
                continue
            wd[k] = v
            m[k] = max(m.get(k, 0), v)
        waits = list(m.items())
        if slot is not None:
            slot.count += slot.step
            tok = (slot.key, slot.count)
            incspec = (slot.sem, slot.step)
        elif inc:
            self.cnt[eng] += 1
            tok = (eng, self.cnt[eng])
            incspec = (self.semof[eng], 1)
        else:
            tok = (eng, self.cnt[eng] + 1)
            incspec = None
        for r in reads:
            st = self.state.setdefault(r, [None, []])
            st[1].append(tok)
            if len(st[1]) > 24:
                mm = {}
                for k, v in st[1]:
                    mm[k] = max(mm.get(k, 0), v)
                st[1] = list(mm.items())
        for w in writes:
            self.state[w] = [tok, []]
        self.streams[eng].append((waits, fn, incspec))
        return tok

    def barrier(self):
        alltoks = {}
        for k in self.ENG:
            if self.cnt[k] > 0:
                alltoks[k] = self.cnt[k]
        for st in self.state.values():
            toks = list(st[1])
            if st[0] is not None:
                toks.append(st[0])
            for (k, v) in toks:
                if k.startswith("dma_"):
                    alltoks[k] = max(alltoks.get(k, 0), v)
        for eng in self.ENG:
            wd = self.waited[eng]
            waits = []
            for k, v in alltoks.items():
                if k == eng or wd.get(k, 0) >= v:
                    continue
                wd[k] = v
                waits.append((k, v))
            if waits:
                self.streams[eng].append((waits, None, None))

    def final_wait(self, eng, toks):
        m = {}
        for k, v in toks:
            m[k] = max(m.get(k, 0), v)
        self.streams[eng].append((list(m.items()), None, None))

    def replay(self):
        nc = self.nc
        sch = self

        def run(name, e):
            for waits, fn, incspec in sch.streams[name]:
                for (k, v) in waits:
                    e.wait_ge(sch.semof[k], v)
                if fn is None:
                    continue
                ins = fn(e)
                if incspec is not None:
                    ins.then_inc(incspec[0], incspec[1])

        with nc.Block() as block:
            @block.tensor
            def _(e):
                run("pe", e)

            @block.vector
            def _(e):
                run("dve", e)

            @block.scalar
            def _(e):
                run("act", e)

            @block.gpsimd
            def _(e):
                run("pool", e)

            @block.sync
            def _(e):
                run("sp", e)


class Arena:
    def __init__(self, nc, nbytes):
        self.t = nc.alloc_sbuf_tensor("arena", [128, nbytes // 4], F32)
        self.cap = nbytes
        self.off = 0

    def take(self, shape, dtype):
        n = int(np.prod(shape))
        sz = n * (4 if dtype == F32 else 2)
        sz = (sz + 31) // 32 * 32
        assert self.off + sz <= self.cap, ("arena overflow", self.off, sz, self.cap)
        v = self.t[:, self.off // 4:(self.off + sz) // 4]
        if dtype != F32:
            v = v.bitcast(dtype)
        v = v[:, 0:n]
        if len(shape) == 2:
            v = v.rearrange("p (a b) -> p a b", a=shape[0])
        elif len(shape) == 3:
            v = v.rearrange("p (a b c) -> p a b c", a=shape[0], b=shape[1])
        self.off += sz
        return v


C_ID, C_TRI, C_NEGM, C_STRICT, C_MASK, C_SAMEC, C_ONES, C_INVF, C_MISC, C_GNW, C_RNW, NCST = (
    0, 128, 256, 384, 512, 640, 768, 896, 960, 976, 1232, 1488)
M_POSC, M_ALOG, M_DTB, M_LG, M_POS4, M_CSEL, M_EPS, M_ONE = 0, 1, 2, 3, 4, 8, 10, 11

TWO_PI = 2.0 * np.pi
CW1 = 6.28125
CW2 = float(TWO_PI - 6.28125)
MAGIC = 12582912.0


def build(SEQ, debug=False):
    B = 2
    NTOK = B * SEQ
    TPC = NTOK // NCORES
    NBLK = NTOK // 512
    BPB = SEQ // 512
    nc = bass.Bass("TRN2", target_bir_lowering=False)
    S = Sched(nc)

    def din(name, shape, dt=F32):
        return nc.dram_tensor(name, list(shape), dt, kind="ExternalInput").ap()

    x = din("x", [NTOK, D])
    xb = din("xb", [TPC, D])
    wq_d = din("wq", [D, 512])
    wt_d = din("wt", [D, 1536])
    wba_d = din("wba", [D, 2])
    cw_d = din("cw", [128, 16])
    n1_d = din("n1", [128, 16])
    n2_d = din("n2", [128, 16])
    nf_d = din("nf", [128, D])
    cst_d = din("cst", [128, NCST])
    wo_d = din("w_out", [D, D])
    wg_d = din("w_gate", [D, DFF])
    wu_d = din("w_up", [D, DFF])
    wd_d = din("w_down", [DFF, D])
    out_d = nc.dram_tensor("out", [TPC, D], F32, kind="ExternalOutput").ap()
    ag_in = nc.dram_tensor("ag_in", [256, NTOK], BF16).ap()
    ag_out = nc.dram_tensor("ag_out", [2048, NTOK], BF16).ap()
    dbg_mixed = None
    if debug:
        dbg_mixed = nc.dram_tensor("dbg_mixed", [256, NTOK], BF16, kind="ExternalOutput").ap()

    S.set_banks({"pt": 0, "ptA": 5, "ptB": 5, "ptC": 6, "ptD": 6, "pGCB": 3, "pKK": 3, "pQK": 3, "pSC": 3,
                 "pN": 4, "pNT": 4, "pX": 4, "pgs": 4, "p1": 5, "p2": 5, "p5": 6, "pU": 6, "pOa": 7, "pOb": 7,
                 "pp0": 0, "pp1": 1, "pp2": 2, "pp3": 3, "pp4": 4, "pp5": 5, "pp6": 6, "pp7": 7})
    A = Arena(nc, 208000)
    banks = [nc.alloc_psum_tensor("pb%d" % i, [128, 512], F32) for i in range(8)]

    def pf(b, off, n):
        return banks[b][:, off:off + n]

    def pb16(b, off, n):
        return banks[b][:, :].bitcast(BF16)[:, off:off + n]

    cst = A.take([NCST], F32)
    identb = A.take([128], BF16)
    Wq = A.take([16, 512], BF16)
    Wt = A.take([16, 1536], BF16)
    Wba = A.take([16, 2], BF16)
    cw = A.take([16], F32)
    n1 = A.take([16], F32)
    cols = A.take([16], F32)
    XI, XIK, GC, NEGA, DTB = 0, 1, 2, 3, 4
    ident = cst[:, C_ID:C_ID + 128]
    tri = cst[:, C_TRI:C_TRI + 128]
    negm = cst[:, C_NEGM:C_NEGM + 128]
    strict = cst[:, C_STRICT:C_STRICT + 128]
    mask01 = cst[:, C_MASK:C_MASK + 128]
    samec = cst[:, C_SAMEC:C_SAMEC + 128]
    ones = cst[:, C_ONES:C_ONES + 128]
    invf = cst[:, C_INVF:C_INVF + 64]
    misc = cst[:, C_MISC:C_MISC + 16]
    gnw = cst[:, C_GNW:C_GNW + 256]
    rnw = cst[:, C_RNW:C_RNW + 256]

    sl = {n: S.slot(n) for n in ("cst", "small", "small2", "dbg", "xs0", "xs1", "wst0", "wst1", "wst2", "mts", "ag_in")}
    s_cc = S.slot("cc", step=1)

    def dma(eng, out, in_, slot, reads=(), writes=()):
        return S.emit(eng, lambda e: e.dma_start(out=out, in_=in_), reads=reads, writes=writes, slot=slot)

    dma("sp", cst, cst_d, sl["cst"], writes=["cst"])
    dma("sp", cw, cw_d, sl["small"], writes=["cw"])
    dma("sp", n1, n1_d, sl["small2"], writes=["n1"])
    S.emit("dve", lambda e: e.tensor_copy(out=identb, in_=ident), reads=["cst"], writes=["identb"])
    S.emit("dve", lambda e: e.memset(cols, 0.0), writes=["cols"])
    lgc = misc[:, M_LG:M_LG + 1]
    tmpc = A.take([8], F32)
    S.emit("dve", lambda e: e.tensor_scalar(out=tmpc[:, 0:1], in0=misc[:, M_POSC:M_POSC + 1], scalar1=lgc, scalar2=None,
                                            op0=ALU.mult), reads=["cst"], writes=["tmpc"])
    S.emit("dve", lambda e: e.tensor_scalar(out=tmpc[:, 1:2], in0=tmpc[:, 0:1], scalar1=-1.0,
                                            scalar2=float(np.log(128.0 ** -0.5)), op0=ALU.mult, op1=ALU.add),
           reads=["tmpc"], writes=["tmpc"])
    S.emit("dve", lambda e: e.tensor_scalar(out=tmpc[:, 2:3], in0=lgc, scalar1=64.0, scalar2=None, op0=ALU.mult),
           reads=["cst", "tmpc"], writes=["tmpc"])
    S.emit("dve", lambda e: e.tensor_copy(out=tmpc[:, 3:4], in_=misc[:, M_ALOG:M_ALOG + 1]), reads=["cst", "tmpc"],
           writes=["tmpc"])
    S.emit("act", lambda e: e.activation(out=cols[:, 0:4], in_=tmpc[:, 0:4], func=AF.Exp), reads=["tmpc", "cols"],
           writes=["cols"])
    S.emit("dve", lambda e: e.tensor_scalar(out=cols[:, NEGA:NEGA + 1], in0=cols[:, NEGA:NEGA + 1], scalar1=-1.0,
                                            scalar2=None, op0=ALU.mult), reads=["cols"], writes=["cols"])
    S.emit("dve", lambda e: e.tensor_copy(out=cols[:, DTB:DTB + 1], in_=misc[:, M_DTB:M_DTB + 1]), reads=["cst", "cols"],
           writes=["cols"])

    xs = [A.take([D], F32) for _ in range(2)]
    xn = A.take([D], BF16)
    uT = A.take([16, 512], BF16)
    pre = A.take([4, 515], F32)
    cv = A.take([4, 512], F32)
    sq = A.take([2, 512], F32)
    rn = A.take([2, 512], F32)
    ssb = A.take([8], F32)
    szt = A.take([256], F32)
    tgt = A.take([256], F32)
    qk_sb = A.take([2, 4, 128], F32)
    rt1 = A.take([2, 4, 128], F32)
    rt2 = A.take([2, 4, 128], F32)
    ang = A.take([4, 128], F32)
    ang2 = A.take([4, 128], F32)
    SC = A.take([4, 128], F32)
    pos4 = A.take([4], F32)
    bat = A.take([8], F32)
    inter = []
    for p in range(2):
        inter.append(dict(
            qTn=A.take([512], BF16), kTn=A.take([512], BF16), vT=A.take([2, 512], BF16),
            bcol=A.take([4], F32), nbcol=A.take([4], F32), gcol=A.take([4], F32),
            qx=A.take([4, 128], BF16), kz=A.take([4, 128], BF16), vb=A.take([4, 256], BF16),
            Ga=A.take([4, 256], F32), Gb=A.take([4, 256], F32)))
    Sa = A.take([256], F32)
    Sabf = A.take([256], BF16)
    Zs = A.take([256], F32)
    Zbf = A.take([256], BF16)
    kTc = [A.take([128], BF16) for _ in range(2)]
    qgTc = [A.take([128], BF16) for _ in range(2)]
    kdc = [A.take([128], BF16) for _ in range(2)]
    qxTc = [A.take([128], BF16) for _ in range(2)]
    kzc = [A.take([128], BF16) for _ in range(2)]
    zero_list = [("Sa", Sa), ("Sabf", Sabf), ("Zs", Zs), ("Zbf", Zbf)]
    for i in range(2):
        zero_list += [("kTc%d" % i, kTc[i]), ("qgTc%d" % i, qgTc[i]), ("kdc%d" % i, kdc[i]),
                      ("qxTc%d" % i, qxTc[i]), ("kzc%d" % i, kzc[i])]
    gB = A.take([128], F32)
    tmpD = A.take([128], F32)
    Ef = A.take([128], F32)
    Es = A.take([128], F32)
    EGCB = A.take([128], F32)
    Nn = [A.take([128], F32) for _ in range(2)]
    NT = [A.take([128], F32) for _ in range(2)]
    Xm = A.take([128], F32)
    M2bf = A.take([128], BF16)
    attnT = A.take([128], BF16)
    scT = A.take([128], BF16)
    qxT = A.take([128], BF16)
    kzT = A.take([128], BF16)
    vtok = A.take([256], BF16)
    Rbf = A.take([256], BF16)
    vnbf = A.take([256], BF16)
    ta = A.take([256], F32)
    tb = A.take([256], F32)
    mixbf = A.take([256], BF16)
    mTs = A.take([2, 512], BF16)
    gcs = A.take([8], F32)
    gsel = A.take([2], F32)
    ost = A.take([16], F32)
    junk = A.take([256], BF16)
    zero_list += [("Rbf", Rbf), ("vnbf", vnbf)]
    print("phase A arena bytes", A.off)

    for nm, t in zero_list:
        S.emit("pool", lambda e, t=t: e.memset(t, 0.0), writes=[nm])

    wq_v = wq_d.rearrange("(k p) n -> p k n", p=128)
    wt_v = wt_d.rearrange("(k p) n -> p k n", p=128)
    wba_v = wba_d.rearrange("(k p) n -> p k n", p=128)
    for kc in range(16):
        st0 = xs[0][:, 0:1536]
        st1 = xs[1][:, 0:512]
        st2 = xs[1][:, 512:514]
        dma("sp", st0, wt_v[:, kc, :], sl["wst0"], writes=["xs0"])
        dma("sp", st1, wq_v[:, kc, :], sl["wst1"], writes=["xs1"])
        dma("sp", st2, wba_v[:, kc, :], sl["wst2"], writes=["xs1b"])
        n1c = n1[:, kc:kc + 1]
        S.emit("dve", lambda e, kc=kc, n1c=n1c, st0=st0: e.tensor_scalar(out=Wt[:, kc, :], in0=st0, scalar1=n1c, scalar2=None,
                                                                    op0=ALU.mult), reads=["xs0", "n1"], writes=["Wt"])
        S.emit("act", lambda e, kc=kc, n1c=n1c, st1=st1: e.activation(out=Wq[:, kc, :], in_=st1, func=AF.Copy, scale=n1c),
               reads=["xs1", "n1"], writes=["Wq"])
        S.emit("dve", lambda e, kc=kc, n1c=n1c, st2=st2: e.tensor_scalar(out=Wba[:, kc, :], in0=st2, scalar1=n1c, scalar2=None,
                                                                    op0=ALU.mult), reads=["xs1b", "n1"], writes=["Wba"])
    S.emit("dve", lambda e: e.tensor_copy(out=ssb[:, 0:1], in_=cst[:, 0:1]), reads=["cst"], writes=["xs1", "xs1b", "ssb"])

    def stage1(b):
        p = b % 2
        I = inter[p]
        sfx = "_%d" % p
        tok0 = b * 512
        bstart = (b % BPB) * 512
        first_in_seq = (b % BPB == 0)
        if LEVEL < 2:
            return
        for t in range(4):
            xsb = xs[t % 2]
            xk = "xs%d" % (t % 2)
            dma("sp", xsb, x[tok0 + t * 128: tok0 + (t + 1) * 128, :], sl[xk], writes=[xk])
            S.emit("act", lambda e, xsb=xsb, t=t: e.activation(out=xn, in_=xsb, func=AF.Square, accum_out=ssb[:, t:t + 1]),
                   reads=[xk], writes=["xn", "ssb"])
            S.emit("dve", lambda e, t=t: e.tensor_scalar(out=ssb[:, 4 + t:5 + t], in0=ssb[:, t:t + 1], scalar1=1.0 / D,
                                                         scalar2=EPS, op0=ALU.mult, op1=ALU.add), reads=["ssb"], writes=["ssb"])
            S.emit("act", lambda e, t=t: e.activation(out=ssb[:, 4 + t:5 + t], in_=ssb[:, 4 + t:5 + t], func=AF.Ln),
                   reads=["ssb"], writes=["ssb"])
            S.emit("act", lambda e, t=t: e.activation(out=ssb[:, 4 + t:5 + t], in_=ssb[:, 4 + t:5 + t], func=AF.Exp, scale=-0.5),
                   reads=["ssb"], writes=["ssb"])
            S.emit("act", lambda e, xsb=xsb, t=t: e.activation(out=xn, in_=xsb, func=AF.Copy, scale=ssb[:, 4 + t:5 + t]),
                   reads=[xk, "ssb"], writes=["xn"])
            for r in range(4):
                for q in range(4):
                    kc = r * 4 + q
                    S.emit("pe", lambda e, kc=kc, q=q: e.transpose(out=pb16(0, q * 128, 128), in_=xn[:, kc * 128:(kc + 1) * 128],
                                                                    identity=identb),
                           reads=["xn", "identb"], writes=["pt"], inc=(q == 3))
                eng = "dve" if r % 2 == 0 else "act"
                src = pb16(0, 0, 512).rearrange("p (a b) -> p a b", a=4)
                dst = uT[:, r * 4:(r + 1) * 4, t * 128:(t + 1) * 128]
                if eng == "dve":
                    S.emit("dve", lambda e, src=src, dst=dst: e.tensor_copy(out=dst, in_=src), reads=["pt"], writes=["uT"])
                else:
                    S.emit("act", lambda e, src=src, dst=dst: e.activation(out=dst, in_=src, func=AF.Copy), reads=["pt"],
                           writes=["uT"])
            yield
        if LEVEL < 3:
            return
        if first_in_seq:
            S.emit("pool", lambda e: e.memset(pre[:, :, 0:3], 0.0), writes=["pre"])
        for cc in range(4):
            bank = 1 + (cc % 2)
            pk = "pp%d" % bank
            for kc in range(16):
                S.emit("pe", lambda e, kc=kc, cc=cc, bank=bank: e.matmul(pf(bank, 0, 512), lhsT=Wq[:, kc, cc * 128:(cc + 1) * 128],
                                                                          rhs=uT[:, kc, :], start=(kc == 0), stop=(kc == 15)),
                       reads=["Wq", "uT"], writes=[pk], inc=(kc == 15))
            S.emit("act", lambda e, cc=cc, bank=bank: e.activation(out=pre[:, cc, 3:515], in_=pf(bank, 0, 512), func=AF.Copy),
                   reads=[pk], writes=["pre"])
            yield
        for cc in range(4):
            S.emit("dve", lambda e, cc=cc: e.tensor_scalar(out=cv[:, cc, :], in0=pre[:, cc, 0:512], scalar1=cw[:, cc * 4:cc * 4 + 1],
                                                           scalar2=None, op0=ALU.mult), reads=["pre", "cw"], writes=["cv"])
            for j in range(1, 4):
                S.emit("dve", lambda e, cc=cc, j=j: e.scalar_tensor_tensor(out=cv[:, cc, :], in0=pre[:, cc, j:j + 512],
                                                                            scalar=cw[:, cc * 4 + j:cc * 4 + j + 1], in1=cv[:, cc, :],
                                                                            op0=ALU.mult, op1=ALU.add),
                       reads=["pre", "cw", "cv"], writes=["cv"])
        S.emit("pool", lambda e: e.tensor_copy(out=pre[:, :, 0:3], in_=pre[:, :, 512:515]), reads=["pre"], writes=["pre"])
        S.emit("act", lambda e: e.activation(out=cv[:, 0:2, :], in_=cv[:, 0:2, :], func=AF.Silu), reads=["cv"], writes=["cv"])
        S.emit("act", lambda e: e.activation(out=I["vT"], in_=cv[:, 2:4, :], func=AF.Silu), reads=["cv"], writes=["vT" + sfx])
        S.emit("pool", lambda e: e.tensor_tensor(out=sq, in0=cv[:, 0:2, :], in1=cv[:, 0:2, :], op=ALU.mult), reads=["cv"],
               writes=["sq"])
        yield
        for c2 in range(2):
            bank = 1 + c2
            pk = "pp%d" % bank
            S.emit("pe", lambda e, c2=c2, bank=bank: e.matmul(pf(bank, 0, 512), lhsT=ones, rhs=sq[:, c2, :], start=True, stop=True),
                   reads=["cst", "sq"], writes=[pk])
            S.emit("act", lambda e, c2=c2, bank=bank: e.activation(out=rn[:, c2, :], in_=pf(bank, 0, 512), func=AF.Ln, bias=misc[:, M_EPS:M_EPS + 1]),
                   reads=[pk], writes=["rn"])
        S.emit("act", lambda e: e.activation(out=rn, in_=rn, func=AF.Exp, scale=-0.5), reads=["rn"], writes=["rn"])
        S.emit("dve", lambda e: e.scalar_tensor_tensor(out=I["qTn"], in0=cv[:, 0, :], scalar=float(128.0 ** -0.5), in1=rn[:, 0, :],
                                                       op0=ALU.mult, op1=ALU.mult), reads=["cv", "rn"], writes=["qTn" + sfx])
        S.emit("dve", lambda e: e.tensor_tensor(out=I["kTn"], in0=cv[:, 1, :], in1=rn[:, 1, :], op=ALU.mult), reads=["cv", "rn"],
               writes=["kTn" + sfx])
        yield
        if LEVEL < 4:
            return
        S.emit("dve", lambda e: e.tensor_scalar(out=pos4, in0=misc[:, M_POS4:M_POS4 + 4], scalar1=float(bstart), scalar2=None,
                                                op0=ALU.add), reads=["cst"], writes=["pos4"])
        for t in range(4):
            S.emit("dve", lambda e, t=t: e.tensor_scalar(out=ang[:, t, 0:64], in0=invf, scalar1=pos4[:, t:t + 1], scalar2=None,
                                                         op0=ALU.mult), reads=["cst", "pos4"], writes=["ang"])
        S.emit("dve", lambda e: e.tensor_scalar(out=ang[:, :, 64:128], in0=ang[:, :, 0:64], scalar1=float(np.pi / 2), scalar2=None,
                                                op0=ALU.add), reads=["ang"], writes=["ang"])
        S.emit("dve", lambda e: e.tensor_scalar(out=ang2, in0=ang, scalar1=float(1.0 / TWO_PI), scalar2=MAGIC, op0=ALU.mult,
                                                op1=ALU.add), reads=["ang"], writes=["ang2"])
        S.emit("dve", lambda e: e.tensor_scalar(out=ang2, in0=ang2, scalar1=-MAGIC, scalar2=None, op0=ALU.add), reads=["ang2"],
               writes=["ang2"])
        S.emit("dve", lambda e: e.scalar_tensor_tensor(out=ang, in0=ang2, scalar=-CW1, in1=ang, op0=ALU.mult, op1=ALU.add),
               reads=["ang", "ang2"], writes=["ang"])
        S.emit("dve", lambda e: e.scalar_tensor_tensor(out=ang, in0=ang2, scalar=-CW2, in1=ang, op0=ALU.mult, op1=ALU.add),
               reads=["ang", "ang2"], writes=["ang"])
        S.emit("dve", lambda e: e.tensor_scalar(out=ang, in0=ang, scalar1=-3.1415925, scalar2=3.1415925, op0=ALU.max, op1=ALU.min),
               reads=["ang"], writes=["ang"])
        S.emit("act", lambda e: e.activation(out=SC, in_=ang, func=AF.Sin), reads=["ang"], writes=["SC"])
        yield
        if LEVEL < 5:
            return
        for t in range(4):
            ut = lambda kc, t=t: uT[:, kc, t * 128:(t + 1) * 128]
            for kc in range(16):
                for grp in range(3):
                    S.emit("pe", lambda e, kc=kc, grp=grp, ut=ut: e.matmul(pf(grp, 0, 512), lhsT=ut(kc),
                                                                          rhs=Wt[:, kc, grp * 512:(grp + 1) * 512],
                                                                          start=(kc == 0), stop=(kc == 15)),
                           reads=["Wt", "uT"], writes=["pp%d" % grp], inc=(kc == 15))
            for grp in range(3):
                bank = grp
                pk = "pp%d" % bank
                if SUB < 2:
                    pass
                elif grp in (0, 2):
                    if SUB < 3 or SUB in (21, 22):
                        continue
                    G = I["Ga"] if grp == 0 else I["Gb"]
                    gk = ("Ga" if grp == 0 else "Gb") + sfx
                    nw = gnw if grp == 0 else rnw
                    S.emit("act", lambda e, bank=bank: e.activation(out=szt, in_=pf(bank, 0, 256), func=AF.Silu), reads=[pk],
                           writes=["szt"])
                    S.emit("act", lambda e, bank=bank: e.activation(out=tgt, in_=pf(bank, 256, 256), func=AF.Tanh, scale=0.5),
                           reads=[pk], writes=["tgt"])
                    if SUB < 4:
                        continue
                    S.emit("dve", lambda e: e.tensor_scalar(out=tgt, in0=tgt, scalar1=0.5, scalar2=0.5, op0=ALU.mult, op1=ALU.add),
                           reads=["tgt"], writes=["tgt"])
                    S.emit("pool", lambda e, nw=nw: e.tensor_tensor(out=szt, in0=szt, in1=nw, op=ALU.mult), reads=["szt", "cst"],
                           writes=["szt"])
                    S.emit("pool", lambda e, G=G, t=t: e.tensor_tensor(out=G[:, t, :], in0=szt, in1=tgt, op=ALU.mult),
                           reads=["szt", "tgt"], writes=[gk])
                else:
                  if SUB != 21:
                    S.emit("act", lambda e, bank=bank, t=t: e.activation(out=qk_sb[:, 0, t, :], in_=pf(bank, 0, 128), func=AF.Copy,
                                                                         scale=cols[:, XI:XI + 1]), reads=[pk, "cols"],
                           writes=["qk_sb"])
                    S.emit("act", lambda e, bank=bank, t=t: e.activation(out=qk_sb[:, 1, t, :], in_=pf(bank, 128, 128), func=AF.Copy,
                                                                         scale=cols[:, XIK:XIK + 1]), reads=[pk, "cols"],
                           writes=["qk_sb"])
                  if SUB != 22:
                    S.emit("dve", lambda e, bank=bank, t=t: e.tensor_copy(out=I["vb"][:, t, :], in_=pf(bank, 256, 256)), reads=[pk],
                           writes=["vb" + sfx])
                yield
            if LEVEL < 6:
                continue
            bankb = 1 + (t % 2)
            pba = "pp%d" % bankb
            for kc in range(16):
                S.emit("pe", lambda e, kc=kc, ut=ut, bankb=bankb: e.matmul(pf(bankb, 0, 2), lhsT=ut(kc), rhs=Wba[:, kc, :], start=(kc == 0),
                                                              stop=(kc == 15)), reads=["Wba", "uT"], writes=[pba], inc=(kc == 15))
            S.emit("act", lambda e, bankb=bankb: e.activation(out=bat[:, 0:1], in_=pf(bankb, 0, 1), func=AF.Tanh, scale=0.5), reads=[pba],
                   writes=["bat"])
            S.emit("dve", lambda e, t=t: e.tensor_scalar(out=I["bcol"][:, t:t + 1], in0=bat[:, 0:1], scalar1=0.5, scalar2=0.5,
                                                         op0=ALU.mult, op1=ALU.add), reads=["bat"], writes=["bcol" + sfx])
            S.emit("dve", lambda e, t=t: e.tensor_scalar(out=I["nbcol"][:, t:t + 1], in0=bat[:, 0:1], scalar1=-0.5, scalar2=-0.5,
                                                         op0=ALU.mult, op1=ALU.add), reads=["bat"], writes=["nbcol" + sfx])
            S.emit("act", lambda e, bankb=bankb: e.activation(out=bat[:, 1:2], in_=pf(bankb, 1, 1), func=AF.Exp, bias=cols[:, DTB:DTB + 1]),
                   reads=[pba, "cols", "bat"], writes=["bat"])
            S.emit("act", lambda e: e.activation(out=bat[:, 2:3], in_=bat[:, 1:2], func=AF.Ln, bias=misc[:, M_ONE:M_ONE + 1]), reads=["bat"],
                   writes=["bat"])
            S.emit("dve", lambda e, t=t: e.tensor_scalar(out=I["gcol"][:, t:t + 1], in0=bat[:, 2:3], scalar1=cols[:, NEGA:NEGA + 1],
                                                         scalar2=None, op0=ALU.mult), reads=["bat", "cols"], writes=["gcol" + sfx])
            yield
        if LEVEL < 7:
            return
        for w in range(2):
            src = qk_sb[:, w, :, :]
            dst = I["qx"] if w == 0 else I["kz"]
            dk_ = ("qx" if w == 0 else "kz") + sfx
            t1 = rt1[:, w, :, :]
            t2 = rt2[:, w, :, :]
            sin = SC[:, :, 0:64]
            cos = SC[:, :, 64:128]
            S.emit("dve", lambda e, src=src, t1=t1, cos=cos: e.tensor_tensor(out=t1[:, :, 0:64], in0=src[:, :, 0:64], in1=cos,
                                                                              op=ALU.mult), reads=["qk_sb", "SC"], writes=["rt1"])
            S.emit("dve", lambda e, src=src, t1=t1, cos=cos: e.tensor_tensor(out=t1[:, :, 64:128], in0=src[:, :, 64:128], in1=cos,
                                                                              op=ALU.mult), reads=["qk_sb", "SC"], writes=["rt1"])
            S.emit("pool", lambda e, src=src, t2=t2, sin=sin: e.tensor_tensor(out=t2[:, :, 0:64], in0=src[:, :, 64:128], in1=sin,
                                                                               op=ALU.mult), reads=["qk_sb", "SC"], writes=["rt2"])
            S.emit("pool", lambda e, src=src, t2=t2, sin=sin: e.tensor_tensor(out=t2[:, :, 64:128], in0=src[:, :, 0:64], in1=sin,
                                                                               op=ALU.mult), reads=["qk_sb", "SC"], writes=["rt2"])
            S.emit("dve", lambda e, dst=dst, t1=t1, t2=t2: e.tensor_tensor(out=dst[:, :, 0:64], in0=t1[:, :, 0:64], in1=t2[:, :, 0:64],
                                                                            op=ALU.subtract), reads=["rt1", "rt2"], writes=[dk_])
            S.emit("dve", lambda e, dst=dst, t1=t1, t2=t2: e.tensor_tensor(out=dst[:, :, 64:128], in0=t1[:, :, 64:128],
                                                                            in1=t2[:, :, 64:128], op=ALU.add),
                   reads=["rt1", "rt2"], writes=[dk_])
        yield

    def stage2(b):
        p = b % 2
        I = inter[p]
        sfx = "_%d" % p
        tok0 = b * 512
        if b % BPB == 0:
            for nm, t in (("Sa", Sa), ("Sabf", Sabf), ("Zs", Zs), ("Zbf", Zbf)):
                S.emit("pool", lambda e, t=t: e.memset(t, 0.0), writes=[nm])
        for t in range(4):
            cs = slice(t * 128, (t + 1) * 128)
            kT = I["kTn"][:, cs]
            qT = I["qTn"][:, cs]
            gcol = I["gcol"][:, t:t + 1]
            bcol = I["bcol"][:, t:t + 1]
            nbcol = I["nbcol"][:, t:t + 1]
            rI = ["kTn" + sfx, "qTn" + sfx, "vT" + sfx, "gcol" + sfx, "bcol" + sfx, "nbcol" + sfx]
            S.emit("dve", lambda e, gcol=gcol: e.tensor_scalar(out=gB, in0=ones, scalar1=gcol, scalar2=None, op0=ALU.mult),
                   reads=["cst", "gcol" + sfx], writes=["gB"])
            S.emit("dve", lambda e, gcol=gcol: e.tensor_scalar(out=gsel, in0=misc[:, M_CSEL:M_CSEL + 2], scalar1=gcol, scalar2=None,
                                                               op0=ALU.mult), reads=["cst", "gcol" + sfx], writes=["gsel"])
            S.emit("pe", lambda e, gcol=gcol: e.matmul(pf(4, 384, 1), lhsT=tri, rhs=gcol, start=True, stop=True),
                   reads=["cst", "gcol" + sfx], writes=["pgs"], inc=False)
            S.emit("pe", lambda e, gcol=gcol: e.matmul(pf(4, 385, 1), lhsT=samec, rhs=gcol, start=True, stop=True),
                   reads=["cst", "gcol" + sfx], writes=["pgs"], inc=False)
            S.emit("pe", lambda e: e.matmul(pf(4, 386, 2), lhsT=ones, rhs=gsel, start=True, stop=True),
                   reads=["cst", "gsel"], writes=["pgs"])
            S.emit("pe", lambda e: e.matmul(pf(3, 0, 128), lhsT=gB, rhs=tri, start=True, stop=True), reads=["gB", "cst"],
                   writes=["pGCB"])
            S.emit("pe", lambda e, kT=kT: e.matmul(pf(3, 128, 128), lhsT=kT, rhs=kT, start=True, stop=True), reads=rI[0:1],
                   writes=["pKK"])
            S.emit("pe", lambda e, kT=kT, qT=qT: e.matmul(pf(3, 256, 128), lhsT=kT, rhs=qT, start=True, stop=True), reads=rI[0:2],
                   writes=["pQK"])
            S.emit("dve", lambda e: e.tensor_copy(out=gcs[:, 0:4], in_=pf(4, 384, 4)), reads=["pgs"], writes=["gcs"])
            S.emit("dve", lambda e: e.tensor_tensor(out=gcs[:, 6:7], in0=gcs[:, 1:2], in1=gcs[:, 0:1], op=ALU.subtract),
                   reads=["gcs"], writes=["gcs"])
            S.emit("act", lambda e: e.activation(out=gcs[:, 4:5], in_=gcs[:, 0:1], func=AF.Exp), reads=["gcs"], writes=["gcs"])
            S.emit("act", lambda e: e.activation(out=gcs[:, 2:4], in_=gcs[:, 2:4], func=AF.Exp), reads=["gcs"], writes=["gcs"])
            S.emit("act", lambda e: e.activation(out=gcs[:, 6:7], in_=gcs[:, 6:7], func=AF.Exp), reads=["gcs"], writes=["gcs"])
            S.emit("dve", lambda e: e.tensor_scalar(out=gcs[:, 5:6], in0=gcs[:, 4:5], scalar1=-1.0, scalar2=None, op0=ALU.mult),
                   reads=["gcs"], writes=["gcs"])
            S.emit("dve", lambda e: e.scalar_tensor_tensor(out=tmpD, in0=pf(3, 0, 128), scalar=gcs[:, 0:1], in1=negm,
                                                           op0=ALU.subtract, op1=ALU.add), reads=["pGCB", "gcs", "cst"],
                   writes=["tmpD"])
            S.emit("act", lambda e: e.activation(out=Ef, in_=tmpD, func=AF.Exp), reads=["tmpD"], writes=["Ef"])
            S.emit("act", lambda e: e.activation(out=EGCB, in_=pf(3, 0, 128), func=AF.Exp), reads=["pGCB"], writes=["EGCB"])
            S.emit("pool", lambda e: e.tensor_tensor(out=Es, in0=Ef, in1=strict, op=ALU.mult), reads=["Ef", "cst"], writes=["Es"])
            S.emit("dve", lambda e, nbcol=nbcol: e.scalar_tensor_tensor(out=Nn[0], in0=pf(3, 128, 128), scalar=nbcol, in1=Es,
                                                                        op0=ALU.mult, op1=ALU.mult),
                   reads=["pKK", "nbcol" + sfx, "Es"], writes=["N0"])
            S.emit("dve", lambda e: e.tensor_tensor(out=attnT, in0=pf(3, 256, 128), in1=Ef, op=ALU.mult), reads=["pQK", "Ef"],
                   writes=["attnT"])
            S.emit("pe", lambda e: e.transpose(out=pf(4, 0, 128), in_=Nn[0], identity=ident), reads=["N0", "cst"], writes=["pN"])
            S.emit("act", lambda e: e.activation(out=NT[0], in_=pf(4, 0, 128), func=AF.Copy), reads=["pN"], writes=["NT0"])
            S.emit("pool", lambda e: e.tensor_tensor(out=Xm, in0=Nn[0], in1=ident, op=ALU.add), reads=["N0", "cst"], writes=["Xm"])
            yield
            for r in range(5):
                a, bb = r % 2, (r + 1) % 2
                last = (r == 4)
                if not last:
                    S.emit("pe", lambda e, a=a: e.matmul(pf(4, 0, 128), lhsT=NT[a], rhs=Nn[a], start=True, stop=True),
                           reads=["NT%d" % a, "N%d" % a], writes=["pN"])
                S.emit("pe", lambda e, a=a: e.matmul(pf(4, 128, 128), lhsT=Nn[a], rhs=NT[a], start=True, stop=True),
                       reads=["NT%d" % a, "N%d" % a], writes=["pNT"])
                if not last:
                    S.emit("act", lambda e, bb=bb: e.activation(out=Nn[bb], in_=pf(4, 0, 128), func=AF.Copy), reads=["pN"],
                           writes=["N%d" % bb])
                S.emit("dve", lambda e, bb=bb: e.tensor_copy(out=NT[bb], in_=pf(4, 128, 128)), reads=["pNT"], writes=["NT%d" % bb])
                S.emit("pe", lambda e, bb=bb: e.matmul(pf(4, 256, 128), lhsT=NT[bb], rhs=Xm, start=True, stop=True),
                       reads=["NT%d" % bb, "Xm"], writes=["pX"])
                if not last:
                    S.emit("dve", lambda e: e.tensor_tensor(out=Xm, in0=pf(4, 256, 128), in1=Xm, op=ALU.add), reads=["pX", "Xm"],
                           writes=["Xm"])
                else:
                    S.emit("dve", lambda e: e.tensor_tensor(out=M2bf, in0=pf(4, 256, 128), in1=Xm, op=ALU.add), reads=["pX", "Xm"],
                           writes=["M2bf"])
                yield
            S.emit("pe", lambda e, kT=kT: e.transpose(out=pb16(5, 0, 128), in_=kT, identity=identb), reads=rI[0:1] + ["identb"],
                   writes=["ptA"])
            for c in range(2):
                rows = slice(c * 64, (c + 1) * 64)
                S.emit("act", lambda e, c=c, rows=rows: e.activation(out=kdc[c][rows, :], in_=pb16(5, 0, 128)[rows, :], func=AF.Copy,
                                                                     scale=gcs[rows, 6:7]), reads=["ptA", "gcs"], writes=["kdc%d" % c])
            for h in range(2):
                S.emit("pe", lambda e, h=h, cs=cs: e.transpose(out=pb16(5, 512 + h * 128, 128), in_=I["vT"][:, h, cs], identity=identb),
                       reads=["vT" + sfx, "identb"], writes=["ptB"], inc=(h == 1))
            S.emit("dve", lambda e: e.tensor_copy(out=vtok, in_=pb16(5, 512, 256)), reads=["ptB"], writes=["vtok"])
            for c in range(2):
                ccs = slice(c * 64, (c + 1) * 64)
                S.emit("pool", lambda e, c=c, ccs=ccs, kT=kT: e.tensor_copy(out=kTc[c][:, ccs], in_=kT[:, ccs]), reads=rI[0:1],
                       writes=["kTc%d" % c])
                S.emit("dve", lambda e, c=c, ccs=ccs, qT=qT: e.tensor_tensor(out=qgTc[c][:, ccs], in0=qT[:, ccs], in1=EGCB[:, ccs],
                                                                            op=ALU.mult), reads=rI[1:2] + ["EGCB"],
                       writes=["qgTc%d" % c])
            yield
            S.emit("pe", lambda e, t=t: e.transpose(out=pb16(6, 0, 128), in_=I["qx"][:, t, :], identity=identb),
                   reads=["qx" + sfx, "identb"], writes=["ptD"])
            S.emit("pe", lambda e, t=t: e.transpose(out=pb16(6, 512, 128), in_=I["kz"][:, t, :], identity=identb),
                   reads=["kz" + sfx, "identb"], writes=["ptC"])
            S.emit("act", lambda e: e.activation(out=qxT, in_=pb16(6, 0, 128), func=AF.Copy), reads=["ptD"], writes=["qxT"])
            S.emit("dve", lambda e: e.tensor_copy(out=kzT, in_=pb16(6, 512, 128)), reads=["ptC"], writes=["kzT"])
            for c in range(2):
                ccs = slice(c * 64, (c + 1) * 64)
                S.emit("pool", lambda e, c=c, ccs=ccs: e.tensor_copy(out=qxTc[c][:, ccs], in_=qxT[:, ccs]), reads=["qxT"],
                       writes=["qxTc%d" % c])
                S.emit("pool", lambda e, c=c, ccs=ccs, t=t: e.tensor_copy(out=kzc[c][ccs, :], in_=I["kz"][ccs, t, :]),
                       reads=["kz" + sfx], writes=["kzc%d" % c])
            S.emit("pe", lambda e: e.matmul(pf(3, 384, 128), lhsT=kzT, rhs=qxT, start=True, stop=True), reads=["kzT", "qxT"],
                   writes=["pSC"])
            S.emit("dve", lambda e: e.tensor_tensor(out=scT, in0=pf(3, 384, 128), in1=mask01, op=ALU.mult), reads=["pSC", "cst"],
                   writes=["scT"])
            yield
            vbt = I["vb"][:, t, :]
            for c in range(2):
                rows = slice(c * 64, (c + 1) * 64)
                S.emit("pe", lambda e, c=c: e.matmul(pf(7, 0, 256), lhsT=qgTc[c], rhs=Sabf, start=(c == 0), stop=False),
                       reads=["qgTc%d" % c, "Sabf"], writes=["pOa"])
                S.emit("pe", lambda e, c=c: e.matmul(pf(7, 256, 256), lhsT=qxTc[c], rhs=Zbf, start=False, stop=False),
                       reads=["qxTc%d" % c, "Zbf"], writes=["pOb"])
                S.emit("pe", lambda e, c=c: e.matmul(pf(5, 0, 256), lhsT=kTc[c], rhs=Sabf, start=True, stop=True),
                       reads=["kTc%d" % c, "Sabf"], writes=["p1"])
                S.emit("dve", lambda e, rows=rows: e.scalar_tensor_tensor(out=Rbf[rows, :], in0=pf(5, 0, 256)[rows, :],
                                                                          scalar=gcs[rows, 5:6], in1=vtok[rows, :], op0=ALU.mult,
                                                                          op1=ALU.add), reads=["p1", "gcs", "vtok"], writes=["Rbf"])
                S.emit("pe", lambda e: e.matmul(pf(5, 256, 256), lhsT=M2bf, rhs=Rbf, start=True, stop=True), reads=["M2bf", "Rbf"],
                       writes=["p2"])
                S.emit("act", lambda e, rows=rows, bcol=bcol: e.activation(out=vnbf[rows, :], in_=pf(5, 256, 256)[rows, :],
                                                                            func=AF.Copy, scale=bcol[rows, :]),
                       reads=["p2", "bcol" + sfx], writes=["vnbf"])
                S.emit("pe", lambda e, c=c: e.matmul(pf(6, 0, 256), lhsT=kdc[c], rhs=vnbf, start=True, stop=True),
                       reads=["kdc%d" % c, "vnbf"], writes=["p5"])
                S.emit("dve", lambda e, c=c: e.scalar_tensor_tensor(out=Sa, in0=Sa, scalar=gcs[:, 2 + c:3 + c], in1=pf(6, 0, 256),
                                                                    op0=ALU.mult, op1=ALU.add), reads=["p5", "gcs", "Sa"],
                       writes=["Sa"])
                S.emit("act", lambda e: e.activation(out=Sabf, in_=Sa, func=AF.Copy), reads=["Sa"], writes=["Sabf"])
                S.emit("pe", lambda e, c=c, vbt=vbt: e.matmul(pf(6, 256, 256), lhsT=kzc[c], rhs=vbt, start=True, stop=True),
                       reads=["kzc%d" % c, "vb" + sfx], writes=["pU"])
                S.emit("dve", lambda e: e.scalar_tensor_tensor(out=Zs, in0=Zs, scalar=cols[:, GC:GC + 1], in1=pf(6, 256, 256),
                                                               op0=ALU.mult, op1=ALU.add), reads=["pU", "cols", "Zs"], writes=["Zs"])
                S.emit("pool", lambda e: e.tensor_scalar(out=Zbf, in0=Zs, scalar1=cols[:, GC:GC + 1], scalar2=None, op0=ALU.mult),
                       reads=["Zs", "cols"], writes=["Zbf"])
                yield
            S.emit("pe", lambda e, vbt=vbt: e.matmul(pf(7, 256, 256), lhsT=scT, rhs=vbt, start=False, stop=False),
                   reads=["scT", "vb" + sfx], writes=["pOb"])
            S.emit("pe", lambda e: e.matmul(pf(7, 0, 256), lhsT=attnT, rhs=vnbf, start=False, stop=True), reads=["attnT", "vnbf"],
                   writes=["pOa"])
            S.emit("act", lambda e: e.activation(out=junk, in_=pf(7, 0, 256), func=AF.Square, accum_out=ost[:, 0:1]),
                   reads=["pOa"], writes=["junk", "ost"])
            S.emit("act", lambda e: e.activation(out=junk, in_=pf(7, 256, 256), func=AF.Square, accum_out=ost[:, 1:2]),
                   reads=["pOb", "ost"], writes=["junk", "ost"])
            S.emit("dve", lambda e: e.tensor_reduce(out=ost[:, 2:3], in_=pf(7, 256, 256), axis=AX.X, op=ALU.add), reads=["pOb", "ost"],
                   writes=["ost"])
            S.emit("dve", lambda e: e.tensor_scalar(out=ost[:, 3:4], in0=ost[:, 2:3], scalar1=1.0 / 256, scalar2=None, op0=ALU.mult),
                   reads=["ost"], writes=["ost"])
            S.emit("dve", lambda e: e.tensor_scalar(out=ost[:, 4:5], in0=ost[:, 0:1], scalar1=1.0 / 256, scalar2=EPS, op0=ALU.mult,
                                                    op1=ALU.add), reads=["ost"], writes=["ost"])
            S.emit("dve", lambda e: e.tensor_tensor(out=ost[:, 6:7], in0=ost[:, 3:4], in1=ost[:, 3:4], op=ALU.mult), reads=["ost"],
                   writes=["ost"])
            S.emit("dve", lambda e: e.tensor_scalar(out=ost[:, 5:6], in0=ost[:, 1:2], scalar1=1.0 / 256, scalar2=EPS, op0=ALU.mult,
                                                    op1=ALU.add), reads=["ost"], writes=["ost"])
            S.emit("dve", lambda e: e.tensor_tensor(out=ost[:, 5:6], in0=ost[:, 5:6], in1=ost[:, 6:7], op=ALU.subtract),
                   reads=["ost"], writes=["ost"])
            S.emit("act", lambda e: e.activation(out=ost[:, 8:10], in_=ost[:, 4:6], func=AF.Ln), reads=["ost"], writes=["ost"])
            S.emit("act", lambda e: e.activation(out=ost[:, 8:10], in_=ost[:, 8:10], func=AF.Exp, scale=-0.5), reads=["ost"],
                   writes=["ost"])
            S.emit("dve", lambda e: e.scalar_tensor_tensor(out=ost[:, 10:11], in0=ost[:, 3:4], scalar=-1.0, in1=ost[:, 9:10],
                                                           op0=ALU.mult, op1=ALU.mult), reads=["ost"], writes=["ost"])
            S.emit("act", lambda e: e.activation(out=ta, in_=pf(7, 0, 256), func=AF.Copy, scale=ost[:, 8:9]), reads=["pOa", "ost"],
                   writes=["ta"])
            S.emit("act", lambda e: e.activation(out=tb, in_=pf(7, 256, 256), func=AF.Identity, scale=ost[:, 9:10],
                                                 bias=ost[:, 10:11]), reads=["pOb", "ost"], writes=["tb"])
            S.emit("pool", lambda e, t=t: e.tensor_tensor(out=ta, in0=ta, in1=I["Ga"][:, t, :], op=ALU.mult),
                   reads=["ta", "Ga" + sfx], writes=["ta"])
            S.emit("pool", lambda e, t=t: e.tensor_tensor(out=tb, in0=tb, in1=I["Gb"][:, t, :], op=ALU.mult),
                   reads=["tb", "Gb" + sfx], writes=["tb"])
            S.emit("dve", lambda e: e.tensor_tensor(out=mixbf, in0=ta, in1=tb, op=ALU.add), reads=["ta", "tb"], writes=["mixbf"])
            for h in range(2):
                S.emit("pe", lambda e, h=h: e.transpose(out=pb16(5, 512 + h * 128, 128), in_=mixbf[:, h * 128:(h + 1) * 128],
                                                         identity=identb), reads=["mixbf", "identb"], writes=["ptB"], inc=(h == 1))
            S.emit("act", lambda e, t=t: e.activation(out=mTs[:, :, t * 128:(t + 1) * 128],
                                                      in_=pb16(5, 512, 256).rearrange("p (a b) -> p a b", a=2), func=AF.Copy),
                   reads=["ptB"], writes=["mTs"])
            yield
        dst = ag_in.rearrange("(h p) n -> p h n", p=128)[:, :, tok0:tok0 + 512]
        dma("sp", dst, mTs, sl["mts"], reads=["mTs"], writes=["ag_in"])
        if debug:
            dstd = dbg_mixed.rearrange("(h p) n -> p h n", p=128)[:, :, tok0:tok0 + 512]
            dma("sp", dstd, mTs, sl["dbg"], reads=["mTs"])
        yield

    def drain(g):
        for _ in g:
            pass

    g1 = stage1(0)
    drain(g1)
    for b in range(NBLK if NBLK_LIMIT is None else NBLK_LIMIT):
        g2 = stage2(b)
        g1 = stage1(b + 1) if b + 1 < NBLK else iter(())
        a_done = b_done = False
        while not (a_done and b_done):
            if not b_done:
                try:
                    next(g2)
                except StopIteration:
                    b_done = True
            if not a_done:
                try:
                    next(g1)
                except StopIteration:
                    a_done = True

    S.barrier()
    if not NOCC:
        S.emit("pool", lambda e: e.collective_compute("AllGather", ALU.bypass, replica_groups=[list(range(NCORES))],
                                                      ins=[ag_in.opt()], outs=[ag_out.opt()]),
               reads=["ag_in"], writes=["ag_out"], slot=s_cc)
    S.barrier()

    A.off = 0
    TP = min(512, TPC)
    NPASS = TPC // TP
    NT_ = TP // 128
    NTB = max(1, TP // 512)
    TBW = min(512, TP)
    cstB = A.take([NCST], F32)
    identbB = A.take([128], BF16)
    n2 = A.take([16], F32)
    nfw = A.take([D], F32)
    hacc = A.take([NT_, D], F32)
    hnT = A.take([16, TP], BF16)
    mT = A.take([16, TP], BF16)
    wo = [A.take([16, 512], BF16) for _ in range(2)]
    xnB = A.take([D], BF16)
    stB = A.take([16], F32)
    aT = [A.take([2, TP], BF16) for _ in range(2)]
    sg = A.take([512], F32)
    WG = [A.take([16, 256], BF16) for _ in range(2)]
    WU = [A.take([16, 256], BF16) for _ in range(2)]
    WD = [A.take([2, D], BF16) for _ in range(2)]
    ot = A.take([D], F32)
    print("phase B arena bytes", A.off)
    slB = {n: S.slot(n) for n in ("n2", "nf", "hacc", "mT", "wo0", "wo1", "wg0", "wg1", "wu0", "wu1", "wd0", "wd1", "out")}
    identB_ = identbB
    dma("sp", n2, n2_d, slB["n2"], writes=["n2"])
    dma("sp", nfw, nf_d, slB["nf"], writes=["nfw"])
    wo_v = wo_d.rearrange("(k p) n -> p k n", p=128)
    wg_v = wg_d.rearrange("(k p) n -> p k n", p=128)
    wu_v = wu_d.rearrange("(k p) n -> p k n", p=128)
    wd_v = wd_d.rearrange("(k p) n -> p k n", p=128)
    ag_v = ag_out.rearrange("(k p) n -> p k n", p=128)
    core_tok0 = None
    NDB = DFF // 256
    out_toks = []

    for ps in range(0 if STOPA else NPASS):
        t0 = ps * TP
        dma("sp", hacc, xb[t0:t0 + TP, :].rearrange("(t p) n -> p t n", p=128), slB["hacc"], writes=["hacc"])
        S.emit("sp", lambda e, t0=t0: e.dma_start(out=mT, in_=(ag_v[:, :, bass.ds(e.partition_id() * TPC + t0, TP)] if not STATIC_TOK else ag_v[:, :, t0:t0 + TP])),
               reads=["ag_out"], writes=["mT"], slot=slB["mT"])
        for cb in range(4):
            wb_ = wo[cb % 2]
            wk = "wo%d" % (cb % 2)
            dma("pool", wb_, wo_v[:, :, cb * 512:(cb + 1) * 512], slB[wk], writes=[wk])
            for t in range(NT_):
                bank = 1 + (t % 6)
                pk = "pp%d" % bank
                for kc in range(16):
                    S.emit("pe", lambda e, kc=kc, t=t, bank=bank, wb_=wb_: e.matmul(pf(bank, 0, 512), lhsT=mT[:, kc, t * 128:(t + 1) * 128],
                                                                                    rhs=wb_[:, kc, :], start=(kc == 0), stop=(kc == 15)),
                           reads=["mT", wk], writes=[pk], inc=(kc == 15))
                S.emit("dve", lambda e, t=t, cb=cb, bank=bank: e.tensor_tensor(out=hacc[:, t, cb * 512:(cb + 1) * 512], in0=pf(bank, 0, 512),
                                                                              in1=hacc[:, t, cb * 512:(cb + 1) * 512], op=ALU.add),
                       reads=[pk, "hacc"], writes=["hacc"])
        for t in range(NT_):
            S.emit("act", lambda e, t=t: e.activation(out=xnB, in_=hacc[:, t, :], func=AF.Square, accum_out=stB[:, 0:1]),
                   reads=["hacc"], writes=["xnB", "stB"])
            S.emit("dve", lambda e: e.tensor_scalar(out=stB[:, 1:2], in0=stB[:, 0:1], scalar1=1.0 / D, scalar2=EPS, op0=ALU.mult,
                                                    op1=ALU.add), reads=["stB"], writes=["stB"])
            S.emit("act", lambda e: e.activation(out=stB[:, 1:2], in_=stB[:, 1:2], func=AF.Ln), reads=["stB"], writes=["stB"])
            S.emit("act", lambda e: e.activation(out=stB[:, 1:2], in_=stB[:, 1:2], func=AF.Exp, scale=-0.5), reads=["stB"],
                   writes=["stB"])
            S.emit("act", lambda e, t=t: e.activation(out=xnB, in_=hacc[:, t, :], func=AF.Copy, scale=stB[:, 1:2]),
                   reads=["hacc", "stB"], writes=["xnB"])
            for r in range(4):
                for q in range(4):
                    kc = r * 4 + q
                    S.emit("pe", lambda e, kc=kc, q=q: e.transpose(out=pb16(0, q * 128, 128), in_=xnB[:, kc * 128:(kc + 1) * 128],
                                                                    identity=identB_), reads=["xnB", "identbB"], writes=["pt"],
                           inc=(q == 3))
                for q in range(4):
                    kc = r * 4 + q
                    if q % 2 == 0:
                        S.emit("dve", lambda e, kc=kc, q=q, t=t: e.tensor_scalar(out=hnT[:, kc, t * 128:(t + 1) * 128],
                                                                               in0=pb16(0, q * 128, 128), scalar1=n2[:, kc:kc + 1],
                                                                               scalar2=None, op0=ALU.mult), reads=["pt", "n2"],
                               writes=["hnT"])
                    else:
                        S.emit("act", lambda e, kc=kc, q=q, t=t: e.activation(out=hnT[:, kc, t * 128:(t + 1) * 128],
                                                                            in_=pb16(0, q * 128, 128), func=AF.Copy,
                                                                            scale=n2[:, kc:kc + 1]), reads=["pt", "n2"], writes=["hnT"])
        def load_w(db):
            i = db % 2
            dma("pool", WG[i], wg_v[:, :, db * 256:(db + 1) * 256], slB["wg%d" % i], writes=["WG%d" % i])
            dma("pool", WU[i], wu_v[:, :, db * 256:(db + 1) * 256], slB["wu%d" % i], writes=["WU%d" % i])
            dma("pool", WD[i], wd_v[:, db * 2:db * 2 + 2, :], slB["wd%d" % i], writes=["WD%d" % i])

        load_w(0)
        bk = 0
        for db in range(NDB):
            i = db % 2
            if db + 1 < NDB:
                load_w(db + 1)
            for c in range(2):
                for tb_ in range(NTB):
                    tsl = slice(tb_ * TBW, (tb_ + 1) * TBW)
                    bg = 1 + (bk % 7)
                    bu = 1 + ((bk + 1) % 7)
                    bk += 2
                    for kc in range(16):
                        S.emit("pe", lambda e, kc=kc, c=c, i=i, tsl=tsl, bg=bg: e.matmul(pf(bg, 0, TBW), lhsT=WG[i][:, kc, c * 128:(c + 1) * 128],
                                                                                         rhs=hnT[:, kc, tsl], start=(kc == 0), stop=(kc == 15)),
                               reads=["WG%d" % i, "hnT"], writes=["pp%d" % bg], inc=(kc == 15))
                    for kc in range(16):
                        S.emit("pe", lambda e, kc=kc, c=c, i=i, tsl=tsl, bu=bu: e.matmul(pf(bu, 0, TBW), lhsT=WU[i][:, kc, c * 128:(c + 1) * 128],
                                                                                         rhs=hnT[:, kc, tsl], start=(kc == 0), stop=(kc == 15)),
                               reads=["WU%d" % i, "hnT"], writes=["pp%d" % bu], inc=(kc == 15))
                    S.emit("act", lambda e, bg=bg: e.activation(out=sg[:, 0:TBW], in_=pf(bg, 0, TBW), func=AF.Silu), reads=["pp%d" % bg],
                           writes=["sg"])
                    S.emit("dve", lambda e, bu=bu, c=c, i=i, tsl=tsl: e.tensor_tensor(out=aT[i][:, c, tsl], in0=pf(bu, 0, TBW), in1=sg[:, 0:TBW],
                                                                                     op=ALU.mult), reads=["pp%d" % bu, "sg"],
                           writes=["aT%d" % i])
            for t in range(NT_):
                for cb in range(4):
                    bd = 1 + (bk % 7)
                    bk += 1
                    for c in range(2):
                        S.emit("pe", lambda e, c=c, i=i, t=t, cb=cb, bd=bd: e.matmul(pf(bd, 0, 512), lhsT=aT[i][:, c, t * 128:(t + 1) * 128],
                                                                                     rhs=WD[i][:, c, cb * 512:(cb + 1) * 512], start=(c == 0),
                                                                                     stop=(c == 1)), reads=["aT%d" % i, "WD%d" % i],
                               writes=["pp%d" % bd], inc=(c == 1))
                    eng = "dve" if (t * 4 + cb) % 3 != 2 else "pool"
                    if eng == "dve":
                        S.emit("dve", lambda e, t=t, cb=cb, bd=bd: e.tensor_tensor(out=hacc[:, t, cb * 512:(cb + 1) * 512], in0=pf(bd, 0, 512),
                                                                                  in1=hacc[:, t, cb * 512:(cb + 1) * 512], op=ALU.add),
                               reads=["pp%d" % bd, "hacc"], writes=["hacc"])
                    else:
                        S.emit("dve", lambda e, t=t, cb=cb, bd=bd: e.tensor_tensor(out=hacc[:, t, cb * 512:(cb + 1) * 512], in0=pf(bd, 0, 512),
                                                                                  in1=hacc[:, t, cb * 512:(cb + 1) * 512], op=ALU.add),
                               reads=["pp%d" % bd, "hacc"], writes=["hacc"])
        for t in range(NT_):
            S.emit("act", lambda e, t=t: e.activation(out=xnB, in_=hacc[:, t, :], func=AF.Square, accum_out=stB[:, 2:3]),
                   reads=["hacc"], writes=["xnB", "stB"])
            S.emit("dve", lambda e: e.tensor_scalar(out=stB[:, 3:4], in0=stB[:, 2:3], scalar1=1.0 / D, scalar2=EPS, op0=ALU.mult,
                                                    op1=ALU.add), reads=["stB"], writes=["stB"])
            S.emit("act", lambda e: e.activation(out=stB[:, 3:4], in_=stB[:, 3:4], func=AF.Ln), reads=["stB"], writes=["stB"])
            S.emit("act", lambda e: e.activation(out=stB[:, 3:4], in_=stB[:, 3:4], func=AF.Exp, scale=-0.5), reads=["stB"],
                   writes=["stB"])
            S.emit("dve", lambda e, t=t: e.scalar_tensor_tensor(out=ot, in0=hacc[:, t, :], scalar=stB[:, 3:4], in1=nfw, op0=ALU.mult,
                                                                op1=ALU.mult), reads=["hacc", "stB", "nfw"], writes=["ot"])
            tok = dma("sp", out_d[t0 + t * 128:t0 + (t + 1) * 128, :], ot, slB["out"], reads=["ot"])
            out_toks.append(tok)
    S.final_wait("sp", out_toks + [(sl["mts"].key, sl["mts"].count), (sl["dbg"].key, sl["dbg"].count)])
    S.replay()
    return nc


def _const_table(h, a_log, dt_bias, gdn_norm_w, ret_norm_w):
    c = np.zeros((128, NCST), np.float32)
    p = np.arange(128)
    ch = p // 64
    same = ch[:, None] == ch[None, :]
    c[:, C_ID:C_ID + 128] = np.eye(128, dtype=np.float32)
    c[:, C_TRI:C_TRI + 128] = (same & (p[:, None] <= p[None, :])).astype(np.float32)
    valid = same & (p[None, :] >= p[:, None])
    c[:, C_NEGM:C_NEGM + 128] = np.where(valid, 0.0, -30000.0).astype(np.float32)
    c[:, C_STRICT:C_STRICT + 128] = (same & (p[None, :] > p[:, None])).astype(np.float32)
    c[:, C_MASK:C_MASK + 128] = valid.astype(np.float32)
    c[:, C_SAMEC:C_SAMEC + 128] = same.astype(np.float32)
    c[:, C_ONES:C_ONES + 128] = 1.0
    inv = (np.float32(10000.0) ** (-np.arange(0, 128, 2, dtype=np.float32) / np.float32(128))).astype(np.float32)
    c[:, C_INVF:C_INVF + 64] = inv[None, :]
    m = C_MISC
    c[:, m + M_POSC] = (p % 64) + 1
    c[:, m + M_ALOG] = a_log[h]
    c[:, m + M_DTB] = dt_bias[h]
    c[:, m + M_LG] = np.log1p(-np.exp2(np.float32(-5.0 - h))).astype(np.float32)
    for t in range(4):
        c[:, m + M_POS4 + t] = p + 128 * t
    c[:, m + M_CSEL + 0] = (ch == 0)
    c[:, m + M_CSEL + 1] = (ch == 1)
    c[:, m + M_EPS] = EPS
    c[:, m + M_ONE] = 1.0
    c[:, C_GNW:C_GNW + 256] = gdn_norm_w[None, :]
    c[:, C_RNW:C_RNW + 256] = ret_norm_w[h * 256:(h + 1) * 256][None, :]
    return c


_NC_CACHE = {}


def make_in_maps(x, norm1_w, w_in, conv_w, a_log, dt_bias, gdn_norm_w, ret_norm_w, w_out, norm2_w, w_gate, w_up, w_down,
                 norm_f_w):
    f = np.float32
    x = np.asarray(x, f)
    B, SEQ, _ = x.shape
    NTOK = B * SEQ
    TPC = NTOK // NCORES
    xf = np.ascontiguousarray(x.reshape(NTOK, D))
    w_in0 = np.asarray(w_in, f)[0]
    conv0 = np.asarray(conv_w, f)[0]
    a_log0 = np.asarray(a_log, f)[0]
    dtb0 = np.asarray(dt_bias, f)[0]
    gnw0 = np.asarray(gdn_norm_w, f)[0]
    rnw0 = np.asarray(ret_norm_w, f)[0]
    n1 = np.ascontiguousarray(np.asarray(norm1_w, f)[0].reshape(16, 128).T)
    n2 = np.ascontiguousarray(np.asarray(norm2_w, f)[0].reshape(16, 128).T)
    nf = np.ascontiguousarray(np.tile(np.asarray(norm_f_w, f)[None, :], (128, 1)))
    wo = np.ascontiguousarray(np.asarray(w_out, f)[0])
    wg = np.ascontiguousarray(np.asarray(w_gate, f)[0])
    wu = np.ascontiguousarray(np.asarray(w_up, f)[0])
    wd = np.ascontiguousarray(np.asarray(w_down, f)[0])
    O_Z, O_B, O_A, O_RQ, O_RK, O_RV, O_RG, O_GA, O_GB = 4096, 6144, 6152, 6160, 7184, 8208, 10256, 12304, 14352
    maps = []
    for h in range(NCORES):
        qc = slice(h * 128, (h + 1) * 128)
        kc = slice(1024 + h * 128, 1024 + (h + 1) * 128)
        vc = slice(2048 + h * 256, 2048 + (h + 1) * 256)
        wq = np.ascontiguousarray(np.concatenate([w_in0[:, qc], w_in0[:, kc], w_in0[:, vc]], axis=1))
        cwh = np.concatenate([conv0[:, qc], conv0[:, kc], conv0[:, vc]], axis=1)
        cw = np.ascontiguousarray(cwh.reshape(4, 4, 128).transpose(2, 1, 0).reshape(128, 16))
        s256 = lambda o: slice(o + h * 256, o + (h + 1) * 256)
        s128 = lambda o: slice(o + h * 128, o + (h + 1) * 128)
        wt = np.ascontiguousarray(np.concatenate([
            w_in0[:, s256(O_Z)], w_in0[:, s256(O_GA)],
            w_in0[:, s128(O_RQ)], w_in0[:, s128(O_RK)], w_in0[:, s256(O_RV)],
            w_in0[:, s256(O_RG)], w_in0[:, s256(O_GB)]], axis=1))
        wba = np.ascontiguousarray(np.stack([w_in0[:, O_B + h], w_in0[:, O_A + h]], axis=1))
        maps.append({
            "x": xf, "xb": np.ascontiguousarray(xf[h * TPC:(h + 1) * TPC]),
            "wq": wq, "wt": wt, "wba": wba, "cw": cw, "n1": n1, "n2": n2, "nf": nf,
            "cst": _const_table(h, a_log0, dtb0, gnw0, rnw0),
            "w_out": wo, "w_gate": wg, "w_up": wu, "w_down": wd,
        })
    return maps, (B, SEQ)


def kernel(**inputs):
    maps, (B, SEQ) = make_in_maps(**inputs)
    key = SEQ
    if key not in _NC_CACHE:
        _NC_CACHE[key] = build(SEQ)
    nc = _NC_CACHE[key]
    res = run_bass_kernel_spmd(nc, maps, core_ids=list(range(NCORES)))
    out = np.concatenate([np.asarray(res.results[c]["out"], np.float32) for c in range(NCORES)], axis=0)
    return out.reshape(B, SEQ, D)
```
